# Optimizing a Trainium2 kernel written in Bass

```python
import jax, jax.numpy as jnp
from jax import lax
import numpy as np

D_MODEL = 1024
BATCH = 4
SEQ = 8192
DEPTH = 2

D_FF = 2816
FFN_RESIDUAL = 0.5
SSD_HEADS = 16
SSD_HEAD_DIM = 64
D_SSM = SSD_HEADS * SSD_HEAD_DIM
SSD_GROUPS = 2
D_STATE = 128
CONV_WIDTH = 4
CHUNK = 128
CONV_CH = D_SSM + 2 * SSD_GROUPS * D_STATE
POOL_WINDOWS = (2, 4, 8, 16)
POOL_GROUP = 256
D_POOL = POOL_GROUP * len(POOL_WINDOWS)
IN_PROJ = D_SSM + CONV_CH + SSD_HEADS + D_POOL
D_MIX_EVEN = D_SSM + D_POOL
ATTN_HEADS = 16
ATTN_KV_HEADS = 4
ATTN_GROUP = ATTN_HEADS // ATTN_KV_HEADS
ATTN_HEAD_DIM = 64
WINDOW = 128
BLOCK = 128
QKV_DIM = (ATTN_HEADS + 2 * ATTN_KV_HEADS) * ATTN_HEAD_DIM
MEM_LEN = 256
MEM_HEADS = 4
MEM_HEAD_DIM = D_MODEL // MEM_HEADS
EPS = 1e-6
N_EVEN = (DEPTH + 1) // 2
N_ODD = DEPTH // 2

kernel_name = "hybrid_ssd_pool_swa_macaron_trunk"


def rms_norm(x, gain):
    xf = x.astype(jnp.float32)
    y = xf * lax.rsqrt(jnp.mean(xf * xf, axis=-1, keepdims=True) + EPS)
    return (y * gain.astype(jnp.float32)).astype(x.dtype)


def swiglu(h, wi, wo):
    gate, up = jnp.split(h @ wi, 2, axis=-1)
    return (jax.nn.silu(gate) * up) @ wo


def alibi_slopes(n):
    return jnp.asarray(2.0 ** (-8.0 * (np.arange(n) + 1) / n), dtype=jnp.float32)


def causal_depthwise_conv(u, w, b):
    ch = u.shape[-1]
    y = lax.conv_general_dilated(u, w[:, None, :].astype(u.dtype), window_strides=(1,),
                                 padding=((CONV_WIDTH - 1, 0),),
                                 dimension_numbers=('NWC', 'WIO', 'NWC'),
                                 feature_group_count=ch)
    return y + b


def ssd_chunked(x, dt, a, b_in, c_in):
    bsz, T = x.shape[0], x.shape[1]
    nc = T // CHUNK
    J = SSD_HEADS // SSD_GROUPS
    xr = (x * dt[..., None]).reshape(bsz, nc, CHUNK, SSD_GROUPS, J, SSD_HEAD_DIM)
    adt = (dt * a).reshape(bsz, nc, CHUNK, SSD_GROUPS, J).transpose(0, 1, 3, 4, 2)
    acs = jnp.cumsum(adt, axis=-1)
    br = b_in.reshape(bsz, nc, CHUNK, SSD_GROUPS, D_STATE)
    cr = c_in.reshape(bsz, nc, CHUNK, SSD_GROUPS, D_STATE)
    causal = np.tril(np.ones((CHUNK, CHUNK), dtype=bool))
    seg = jnp.exp(jnp.where(causal, acs[..., :, None] - acs[..., None, :], -jnp.inf))
    cb = jnp.einsum('bclgn,bcsgn->bcgls', cr, br)
    y_diag = jnp.einsum('bcgjls,bcsgjp->bclgjp', cb[:, :, :, None] * seg, xr)
    decay_to_end = jnp.exp(acs[..., -1:] - acs).transpose(0, 1, 4, 2, 3)
    states = jnp.einsum('bclgn,bclgjp->bcgjpn', br, xr * decay_to_end[..., None])
    chunk_decay = jnp.exp(acs[..., -1])

    def carry_state(h, inp):
        st, dec = inp
        return h * dec[..., None, None] + st, h

    _, states_in = lax.scan(carry_state, jnp.zeros_like(states[:, 0]),
                            (jnp.moveaxis(states, 1, 0), jnp.moveaxis(chunk_decay, 1, 0)))
    states_in = jnp.moveaxis(states_in, 0, 1)
    decay_from_start = jnp.exp(acs).transpose(0, 1, 4, 2, 3)
    y_off = jnp.einsum('bclgn,bcgjpn->bclgjp', cr, states_in) * decay_from_start[..., None]
    return (y_diag + y_off).reshape(bsz, T, SSD_HEADS, SSD_HEAD_DIM)


def multiscale_causal_pool(u, pool_w, pool_scale):
    T = u.shape[1]
    t = jnp.arange(T)
    outs = []
    for k, w in enumerate(POOL_WINDOWS):
        ug = u[..., k * POOL_GROUP:(k + 1) * POOL_GROUP].astype(jnp.float32)
        csum = jnp.cumsum(ug, axis=1)
        csum_shift = jnp.pad(csum, ((0, 0), (w, 0), (0, 0)))[:, :T]
        count = jnp.minimum(t + 1, w).astype(jnp.float32)[None, :, None]
        pooled = ((csum - csum_shift) / count - ug).astype(u.dtype)
        outs.append(pooled @ pool_w[k])
    return jnp.concatenate(outs, axis=-1) * pool_scale


def ssd_pool_mixer(h, in_proj, conv_w, conv_b, dt_bias, a_log, d_skip, ssd_norm,
                   pool_w, pool_scale, out_proj):
    bsz, T, _ = h.shape
    proj = h @ in_proj
    z, xbc, dt_raw, u_pool = jnp.split(
        proj, [D_SSM, D_SSM + CONV_CH, D_SSM + CONV_CH + SSD_HEADS], axis=-1)
    xbc = jax.nn.silu(causal_depthwise_conv(xbc, conv_w, conv_b))
    xs, b_in, c_in = jnp.split(xbc, [D_SSM, D_SSM + SSD_GROUPS * D_STATE], axis=-1)
    dt = jax.nn.softplus(dt_raw.astype(jnp.float32) + dt_bias.astype(jnp.float32))
    a = -jnp.exp(a_log.astype(jnp.float32))
    xs_h = xs.reshape(bsz, T, SSD_HEADS, SSD_HEAD_DIM).astype(jnp.float32)
    y = ssd_chunked(xs_h, dt, a,
                    b_in.reshape(bsz, T, SSD_GROUPS, D_STATE).astype(jnp.float32),
                    c_in.reshape(bsz, T, SSD_GROUPS, D_STATE).astype(jnp.float32))
    y = y + d_skip.astype(jnp.float32)[:, None] * xs_h
    y = y.reshape(bsz, T, D_SSM) * jax.nn.silu(z.astype(jnp.float32))
    y = rms_norm(y.reshape(bsz, T, SSD_GROUPS, D_SSM // SSD_GROUPS),
                 ssd_norm.reshape(SSD_GROUPS, D_SSM // SSD_GROUPS))
    y_ssd = y.reshape(bsz, T, D_SSM).astype(h.dtype)
    y_pool = multiscale_causal_pool(u_pool, pool_w, pool_scale)
    return jnp.concatenate([y_ssd, y_pool], axis=-1) @ out_proj


def swa_sink_attention(h, wqkv, bqkv, qnorm, knorm, sinks, wo, bo):
    bsz, T, _ = h.shape
    nb = T // BLOCK
    HD, KVH, G = ATTN_HEAD_DIM, ATTN_KV_HEADS, ATTN_GROUP
    qkv = h @ wqkv + bqkv
    q, k, v = jnp.split(qkv, [ATTN_HEADS * HD, (ATTN_HEADS + KVH) * HD], axis=-1)
    q = rms_norm(q.reshape(bsz, T, KVH, G, HD), qnorm)
    k = rms_norm(k.reshape(bsz, T, KVH, HD), knorm)
    v = v.reshape(bsz, T, KVH, HD)
    qb = q.reshape(bsz, nb, BLOCK, KVH, G, HD)

    def band(t):
        tb = t.reshape(bsz, nb, BLOCK, KVH, HD)
        prev = jnp.pad(tb, ((0, 0), (1, 0), (0, 0), (0, 0), (0, 0)))[:, :nb]
        return jnp.concatenate([prev, tb], axis=2)

    kk, vv = band(k), band(v)
    s = jnp.einsum('bnqhgd,bnkhd->bnhgqk', qb, kk).astype(jnp.float32) * (HD ** -0.5)
    dist = np.arange(BLOCK)[:, None] + BLOCK - np.arange(2 * BLOCK)[None, :]
    in_window = (dist >= 0) & (dist < WINDOW)
    has_prev = (jnp.arange(nb)[:, None, None] > 0) | (np.arange(2 * BLOCK) >= BLOCK)[None, None, :]
    mask = in_window[None] & has_prev
    slopes = alibi_slopes(ATTN_HEADS).reshape(KVH, G)
    s = s - slopes[:, :, None, None] * jnp.asarray(dist, dtype=jnp.float32)
    s = jnp.where(mask[None, :, None, None], s, -jnp.inf)
    sink = jnp.broadcast_to(sinks.astype(jnp.float32).reshape(KVH, G, 1, 1), s.shape[:-1] + (1,))
    p = jax.nn.softmax(jnp.concatenate([s, sink], axis=-1), axis=-1)[..., :-1]
    o = jnp.einsum('bnhgqk,bnkhd->bnqhgd', p.astype(h.dtype), vv)
    return o.reshape(bsz, T, ATTN_HEADS * HD) @ wo + bo


def memory_cross_attention(h, mem_k, mem_v, wq, qnorm, wo):
    bsz, T, _ = h.shape
    q = rms_norm((h @ wq).reshape(bsz, T, MEM_HEADS, MEM_HEAD_DIM), qnorm)
    s = jnp.einsum('bthd,bmhd->bhtm', q, mem_k).astype(jnp.float32) * (MEM_HEAD_DIM ** -0.5)
    p = jax.nn.softmax(s, axis=-1).astype(h.dtype)
    o = jnp.einsum('bhtm,bmhd->bthd', p, mem_v)
    return o.reshape(bsz, T, D_MODEL) @ wo


def setup_inputs(seed: int = 0) -> dict:
    key = jax.random.key(seed)
    ks = iter(jax.random.split(key, 48))
    f32 = jnp.float32

    def nrm(shape, scale):
        return jax.random.normal(next(ks), shape, f32) * scale

    def gain(shape):
        return 1.0 + nrm(shape, 0.05)

    L = DEPTH
    dt0 = jnp.exp(jax.random.uniform(next(ks), (N_EVEN, SSD_HEADS), f32,
                                     np.log(1e-3), np.log(1e-1)))
    dt_bias = dt0 + jnp.log(-jnp.expm1(-dt0))
    a_log = jnp.log(jax.random.uniform(next(ks), (N_EVEN, SSD_HEADS), f32, 1.0, 16.0))
    return {
        "x": nrm((BATCH, SEQ, D_MODEL), 1.0),
        "mem": nrm((BATCH, MEM_LEN, D_MODEL), 1.0),
        "mem_norm": gain((D_MODEL,)),
        "mem_wkv": nrm((D_MODEL, 2 * D_MODEL), D_MODEL ** -0.5),
        "mem_knorm": gain((MEM_HEAD_DIM,)),
        "ffn1_norm": gain((L, D_MODEL)),
        "ffn1_wi": nrm((L, D_MODEL, 2 * D_FF), D_MODEL ** -0.5),
        "ffn1_wo": nrm((L, D_FF, D_MODEL), D_FF ** -0.5),
        "mix_norm": gain((L, D_MODEL)),
        "ssd_in_proj": nrm((N_EVEN, D_MODEL, IN_PROJ), D_MODEL ** -0.5),
        "ssd_conv_w": nrm((N_EVEN, CONV_WIDTH, CONV_CH), CONV_WIDTH ** -0.5),
        "ssd_conv_b": nrm((N_EVEN, CONV_CH), 0.02),
        "ssd_dt_bias": dt_bias,
        "ssd_a_log": a_log,
        "ssd_d": 1.0 + nrm((N_EVEN, SSD_HEADS), 0.1),
        "ssd_norm": gain((N_EVEN, D_SSM)),
        "pool_w": nrm((N_EVEN, len(POOL_WINDOWS), POOL_GROUP, POOL_GROUP), POOL_GROUP ** -0.5),
        "pool_scale": gain((N_EVEN, D_POOL)),
        "even_out_proj": nrm((N_EVEN, D_MIX_EVEN, D_MODEL), D_MIX_EVEN ** -0.5),
        "attn_wqkv": nrm((N_ODD, D_MODEL, QKV_DIM), D_MODEL ** -0.5),
        "attn_bqkv": nrm((N_ODD, QKV_DIM), 0.02),
        "attn_qnorm": gain((N_ODD, ATTN_HEAD_DIM)),
        "attn_knorm": gain((N_ODD, ATTN_HEAD_DIM)),
        "attn_sinks": nrm((N_ODD, ATTN_HEADS), 0.5),
        "attn_wo": nrm((N_ODD, ATTN_HEADS * ATTN_HEAD_DIM, D_MODEL), (ATTN_HEADS * ATTN_HEAD_DIM) ** -0.5),
        "attn_bo": nrm((N_ODD, D_MODEL), 0.02),
        "xattn_norm": gain((L, D_MODEL)),
        "xattn_wq": nrm((L, D_MODEL, D_MODEL), D_MODEL ** -0.5),
        "xattn_qnorm": gain((L, MEM_HEAD_DIM)),
        "xattn_wo": nrm((L, D_MODEL, D_MODEL), D_MODEL ** -0.5),
        "ffn2_norm": gain((L, D_MODEL)),
        "ffn2_wi": nrm((L, D_MODEL, 2 * D_FF), D_MODEL ** -0.5),
        "ffn2_wo": nrm((L, D_FF, D_MODEL), D_FF ** -0.5),
    }


def reference(x, mem, mem_norm, mem_wkv, mem_knorm, ffn1_norm, ffn1_wi, ffn1_wo, mix_norm,
              ssd_in_proj, ssd_conv_w, ssd_conv_b, ssd_dt_bias, ssd_a_log, ssd_d, ssd_norm,
              pool_w, pool_scale, even_out_proj, attn_wqkv, attn_bqkv, attn_qnorm, attn_knorm,
              attn_sinks, attn_wo, attn_bo, xattn_norm, xattn_wq, xattn_qnorm, xattn_wo,
              ffn2_norm, ffn2_wi, ffn2_wo):
    bsz = mem.shape[0]
    mem_kv = rms_norm(mem, mem_norm) @ mem_wkv
    mem_k, mem_v = jnp.split(mem_kv, 2, axis=-1)
    mem_k = rms_norm(mem_k.reshape(bsz, MEM_LEN, MEM_HEADS, MEM_HEAD_DIM), mem_knorm)
    mem_v = mem_v.reshape(bsz, MEM_LEN, MEM_HEADS, MEM_HEAD_DIM)
    for i in range(DEPTH):
        x = x + FFN_RESIDUAL * swiglu(rms_norm(x, ffn1_norm[i]), ffn1_wi[i], ffn1_wo[i])
        h = rms_norm(x, mix_norm[i])
        if i % 2 == 0:
            e = i // 2
            x = x + ssd_pool_mixer(h, ssd_in_proj[e], ssd_conv_w[e], ssd_conv_b[e], ssd_dt_bias[e],
                                   ssd_a_log[e], ssd_d[e], ssd_norm[e], pool_w[e], pool_scale[e],
                                   even_out_proj[e])
        else:
            o = i // 2
            x = x + swa_sink_attention(h, attn_wqkv[o], attn_bqkv[o], attn_qnorm[o], attn_knorm[o],
                                       attn_sinks[o], attn_wo[o], attn_bo[o])
        x = x + memory_cross_attention(rms_norm(x, xattn_norm[i]), mem_k, mem_v,
                                       xattn_wq[i], xattn_qnorm[i], xattn_wo[i])
        x = x + FFN_RESIDUAL * swiglu(rms_norm(x, ffn2_norm[i]), ffn2_wi[i], ffn2_wo[i])
    return x
```

```python
import numpy as np
import concourse.bass as bass
import concourse.mybir as mybir
from concourse.bass_utils import run_bass_kernel_spmd

F32 = mybir.dt.float32
BF16 = mybir.dt.bfloat16
AF = mybir.ActivationFunctionType
ALU = mybir.AluOpType
AX = mybir.AxisListType

ENGS = ("pe", "act", "dve", "pool", "sp")
D = 1024
DFF = 2816
TW = 512
EPT = 2
EPS = 1e-6


class Res:
    __slots__ = ("name", "lw", "rd")

    def __init__(self, name):
        self.name = name
        self.lw = None
        self.rd = {}


class Op:
    __slots__ = ("eng", "fn", "deps", "dma", "sem", "cnt", "idx", "sig", "ord", "waits", "ep")


class Prog:
    def __init__(self, nc):
        self.nc = nc
        self.ops = {e: [] for e in ENGS}
        self.dsem = {}
        self.esem = {}
        self.final = []
        self.nres = 0
        self.auto = None
        self.epoch = 0

    def res(self, name=None):
        self.nres += 1
        return Res(name or f"r{self.nres}")

    def _deps(self, op, reads, writes):
        deps = []
        for r in reads:
            if r.lw is not None:
                deps.append(r.lw)
        for w in writes:
            if w.lw is not None:
                deps.append(w.lw)
            for o in w.rd.values():
                if isinstance(o, list):
                    deps.extend(o)
                else:
                    deps.append(o)
        op.deps = [d for d in deps if d is not op]
        for r in reads:
            if op.dma:
                r.rd.setdefault("dma", []).append(op)
            else:
                r.rd[op.eng] = op
        for w in writes:
            w.lw = op
            w.rd = {}

    def op(self, eng, fn, reads=(), writes=()):
        op = Op()
        op.eng = eng
        op.fn = fn
        op.dma = False
        op.sig = False
        op.sem = None
        op.ep = self.epoch
        op.idx = len(self.ops[eng])
        if self.auto is not None and self.auto not in writes:
            reads = list(reads) + [self.auto]
        self._deps(op, reads, writes)
        self.ops[eng].append(op)
        return op

    def dma(self, eng, out, in_, reads=(), writes=(), semkey=None, final=False, **kw):
        op = Op()
        op.eng = eng
        op.fn = lambda e: e.dma_start(out=out, in_=in_, **kw)
        op.dma = True
        op.sig = False
        op.ep = self.epoch
        op.idx = len(self.ops[eng])
        key = (semkey if semkey is not None else (writes[0] if writes else "misc"), self.epoch)
        if key not in self.dsem:
            self.dsem[key] = [self.nc.alloc_semaphore(f"d{len(self.dsem)}"), 0]
        ent = self.dsem[key]
        ent[1] += 16
        op.sem = ent[0]
        op.cnt = ent[1]
        self._deps(op, reads, writes)
        self.ops[eng].append(op)
        if final:
            self.final.append(op)
        return op

    def emit(self):
        nc = self.nc
        for e in ENGS:
            waited = {}
            for op in self.ops[e]:
                ws = []
                for d in op.deps:
                    if d.dma:
                        k = ("d", id(d.sem))
                        if waited.get(k, 0) < d.cnt:
                            waited[k] = d.cnt
                            ws.append(d)
                    else:
                        if d.eng == "pe" and e == "pe" and not op.dma:
                            continue
                        k = ("e", d.eng)
                        if waited.get(k, -1) < d.idx:
                            waited[k] = d.idx
                            d.sig = True
                            ws.append(d)
                op.waits = ws
        for e in ENGS:
            c = {}
            for op in self.ops[e]:
                if not op.dma and op.sig:
                    c[op.ep] = c.get(op.ep, 0) + 1
                    op.ord = c[op.ep]
                    if (e, op.ep) not in self.esem:
                        self.esem[(e, op.ep)] = nc.alloc_semaphore(f"e_{e}_{op.ep}")
        fin = list(self.final)
        ops = self.ops
        esem = self.esem

        def run(e, eng):
            for op in ops[e]:
                need = {}
                for d in op.waits:
                    if d.dma:
                        k = id(d.sem)
                        if k not in need or need[k][1] < d.cnt:
                            need[k] = (d.sem, d.cnt)
                    else:
                        k = (d.eng, d.ep)
                        if k not in need or need[k][1] < d.ord:
                            need[k] = (esem[k], d.ord)
                for sem, v in need.values():
                    eng.wait_ge(sem, v)
                ins = op.fn(eng)
                if op.dma:
                    ins.then_inc(op.sem, 16)
                elif op.sig:
                    ins.then_inc(esem[(e, op.ep)], 1)
            if e == "sp":
                seen = {}
                for d in fin:
                    seen[id(d.sem)] = (d.sem, max(d.cnt, seen.get(id(d.sem), (None, 0))[1]))
                for sem, v in seen.values():
                    eng.wait_ge(sem, v)

        with nc.Block() as block:
            @block.sync
            def _(eng):
                run("sp", eng)

            @block.tensor
            def _(eng):
                run("pe", eng)

            @block.scalar
            def _(eng):
                run("act", eng)

            @block.vector
            def _(eng):
                run("dve", eng)

            @block.gpsimd
            def _(eng):
                run("pool", eng)


CF = {}
_o = 0
for _n, _w in [("ident", 128), ("ones", 128), ("bones", 128), ("oh0", 128), ("oh1", 128), ("tri", 128), ("ustr", 128),
               ("negd", 256), ("negd1", 256), ("invc", 64), ("flag", 1)]:
    CF[_n] = (_o, _w)
    _o += _w
NCF = _o

PF = {}
_o = 0
for _n, _w in [("gffn1", 16), ("gmix", 16), ("gxat", 16), ("gffn2", 16), ("gmem", 8), ("gmemk", 2), ("gxq", 4),
               ("convw", 48), ("convb", 12), ("pscale", 8), ("bq", 8), ("bk", 2), ("bv", 256), ("gq", 1), ("gk", 1),
               ("bo", 8), ("sink", 8), ("dtb", 16), ("alog", 16), ("dexp", 1024), ("ssdn", 1024)]:
    PF[_n] = (_o, _w)
    _o += _w
NPF = _o

SLOPES = [2.0 ** (-8.0 * (i + 1) / 16) for i in range(16)]


def _qperm():
    idx = np.zeros((8, 128), dtype=np.int64)
    for a in range(2):
        for g in range(4):
            for half in range(2):
                head = (2 * a + half) * 4 + g
                idx[a * 4 + g, half * 64:(half + 1) * 64] = head * 64 + np.arange(64)
    return idx


def make_consts(first):
    c = np.zeros((128, NCF), dtype=np.float32)
    i = np.arange(128)
    c[:, CF["ident"][0]:CF["ident"][0] + 128] = np.eye(128)
    c[:, CF["ones"][0]:CF["ones"][0] + 128] = 1.0
    c[:, CF["bones"][0]:CF["bones"][0] + 128] = (i[:, None] // 64 == i[None, :] // 64)
    c[:, CF["tri"][0]:CF["tri"][0] + 128] = (i[:, None] <= i[None, :])
    c[:, CF["ustr"][0]:CF["ustr"][0] + 128] = (i[:, None] > i[None, :])
    BIG = -30000.0
    nd = np.zeros((128, 2, 128), dtype=np.float32)
    k = i[:, None]
    q = i[None, :]
    d0 = q + 128 - k
    nd[:, 0, :] = np.where(d0 < 128, -d0, BIG)
    d1 = q - k
    nd[:, 1, :] = np.where(d1 >= 0, -d1, BIG)
    c[:, CF["negd"][0]:CF["negd"][0] + 256] = nd.reshape(128, 256)
    nd1 = nd.copy()
    if first:
        nd1[:, 0, :] = BIG
    c[:, CF["negd1"][0]:CF["negd1"][0] + 256] = nd1.reshape(128, 256)
    inv = np.zeros((4, 16), dtype=np.float32)
    for kk, w in enumerate((2, 4, 8, 16)):
        t = np.arange(16)
        inv[kk] = 1.0 / (np.minimum(t + 1, w) if first else w)
    c[:, CF["invc"][0]:CF["invc"][0] + 64] = inv.reshape(1, 64)
    c[:, CF["flag"][0]] = 0.0 if first else 1.0
    c[:, CF["oh0"][0]:CF["oh0"][0] + 64] = 1.0
    c[:, CF["oh1"][0] + 64:CF["oh1"][0] + 128] = 1.0
    return c


def make_params(inp):
    p = np.zeros((128, NPF), dtype=np.float32)

    def put(name, arr):
        o, w = PF[name]
        arr = np.asarray(arr, dtype=np.float32)
        assert arr.shape == (128, w), (name, arr.shape)
        p[:, o:o + w] = arr

    def pp(v):
        v = np.asarray(v)
        return v.reshape(-1, 128).T

    put("gffn1", np.concatenate([pp(inp["ffn1_norm"][l]) for l in range(2)], axis=1))
    put("gmix", np.concatenate([pp(inp["mix_norm"][l]) for l in range(2)], axis=1))
    put("gxat", np.concatenate([pp(inp["xattn_norm"][l]) for l in range(2)], axis=1))
    put("gffn2", np.concatenate([pp(inp["ffn2_norm"][l]) for l in range(2)], axis=1))
    put("gmem", pp(inp["mem_norm"]))
    put("gmemk", pp(inp["mem_knorm"]))
    put("gxq", np.concatenate([pp(inp["xattn_qnorm"][l]) for l in range(2)], axis=1))
    cw = np.asarray(inp["ssd_conv_w"][0])
    put("convw", cw.T.reshape(12, 128, 4).transpose(1, 0, 2).reshape(128, 48))
    put("convb", pp(inp["ssd_conv_b"][0]))
    put("pscale", pp(inp["pool_scale"][0]))
    qi = _qperm()
    bqkv = np.asarray(inp["attn_bqkv"][0])
    put("bq", bqkv[qi].T)
    put("bk", pp(bqkv[1024:1280]))
    put("bv", np.tile(bqkv[1280:1536][None, :], (128, 1)))
    put("gq", np.tile(np.asarray(inp["attn_qnorm"][0]), 2)[:, None])
    put("gk", np.tile(np.asarray(inp["attn_knorm"][0]), 2)[:, None])
    put("bo", pp(inp["attn_bo"][0]))
    sk = np.asarray(inp["attn_sinks"][0])
    put("sink", sk[qi // 64].T)
    put("dtb", np.tile(np.asarray(inp["ssd_dt_bias"][0])[None, :], (128, 1)))
    put("alog", np.tile(np.asarray(inp["ssd_a_log"][0])[None, :], (128, 1)))
    put("dexp", np.tile(np.repeat(np.asarray(inp["ssd_d"][0]), 64)[None, :], (128, 1)))
    put("ssdn", np.tile(np.asarray(inp["ssd_norm"][0])[None, :], (128, 1)))
    return p


def build(NT, nstage=8, split=None):
    nc = bass.Bass("TRN2", target_bir_lowering=False)
    P = Prog(nc)
    NTOK = NT * TW

    def din(name, shape):
        return nc.dram_tensor(name, list(shape), F32, kind="ExternalInput").ap()

    x_d = din("x", [NTOK, D])
    mem_d = din("mem", [256, D])
    cst_d = din("cst", [128, NCF])
    prm_d = din("prm", [128, NPF])
    w_in = {
        "ffn1_wi": din("ffn1_wi", [2, D, 2 * DFF]), "ffn1_wo": din("ffn1_wo", [2, DFF, D]),
        "ffn2_wi": din("ffn2_wi", [2, D, 2 * DFF]), "ffn2_wo": din("ffn2_wo", [2, DFF, D]),
        "ssd_in_proj": din("ssd_in_proj", [D, 3600]), "even_out_proj": din("even_out_proj", [2048, D]),
        "attn_wqkv": din("attn_wqkv", [D, 1536]), "attn_wo": din("attn_wo", [D, D]),
        "xattn_wq": din("xattn_wq", [2, D, D]), "xattn_wo": din("xattn_wo", [2, D, D]),
        "mem_wkv": din("mem_wkv", [D, 2 * D]), "pool_w": din("pool_w", [4, 256, 256]),
    }
    NOUT = NTOK if split is None else split[1] * TW
    out_d = nc.dram_tensor("out", [NOUT, D], F32, kind="ExternalOutput").ap()

    def sb(name, shape, dt=F32):
        return nc.alloc_sbuf_tensor(name, list(shape), dt)

    scr = {}

    def mkscr(name, C, O):
        t = nc.dram_tensor("s_" + name, [128, C, O], BF16, kind="Internal").ap()
        scr[name] = (t, P.res("s_" + name))
        return t

    def mkscr2(name, J):
        t = nc.dram_tensor("s_" + name, [128, 8, J * 128], BF16, kind="Internal").ap()
        scr[name] = (t, P.res("s_" + name))
        return t

    def conv_w2(name, src, J):
        t, r = scr[name]
        for j in range(J):
            P.dma("pool", t[:, :, j * 128:(j + 1) * 128], src[j * 128:(j + 1) * 128, :].rearrange("p (m o) -> p m o", m=8),
                  writes=[r], semkey=r)

    def conv_w(name, src, C):
        t, r = scr[name]
        for c in range(C):
            P.dma("pool", t[:, c, :], src[c * 128:(c + 1) * 128, :], writes=[r], semkey=r)

    order = []
    for l in range(2):
        for f in ("ffn1", "ffn2"):
            mkscr(f"{f}_wi{l}", 8, 2 * DFF)
            mkscr2(f"{f}_wo{l}", 22)
    mkscr("wkv", 8, 2 * D)
    mkscr("inproj", 8, 3600)
    mkscr2("outproj", 16)
    mkscr("wqkv", 8, 1536)
    mkscr("awo", 8, D)
    for l in range(2):
        mkscr(f"xwq{l}", 8, D)
        mkscr(f"xwo{l}", 8, D)
    def issue_conversions():
        conv_w("ffn1_wi0", w_in["ffn1_wi"][0], 8)
        conv_w2("ffn1_wo0", w_in["ffn1_wo"][0], 22)
        conv_w("inproj", w_in["ssd_in_proj"], 8)
        conv_w("wkv", w_in["mem_wkv"], 8)
        conv_w2("outproj", w_in["even_out_proj"], 16)
        conv_w("xwq0", w_in["xattn_wq"][0], 8)
        conv_w("xwo0", w_in["xattn_wo"][0], 8)
        conv_w("ffn2_wi0", w_in["ffn2_wi"][0], 8)
        conv_w2("ffn2_wo0", w_in["ffn2_wo"][0], 22)
        conv_w("ffn1_wi1", w_in["ffn1_wi"][1], 8)
        conv_w2("ffn1_wo1", w_in["ffn1_wo"][1], 22)
        t_qkv, r_qkv = scr["wqkv"]
        for c in range(8):
            for a in range(2):
                for g in range(4):
                    srcq = w_in["attn_wqkv"][c * 128:(c + 1) * 128, a * 512:(a + 1) * 512].rearrange(
                        "p (hf g d) -> p hf g d", hf=2, g=4)[:, :, g, :]
                    dstq = t_qkv[:, c, (a * 4 + g) * 128:(a * 4 + g + 1) * 128].rearrange("p (hf d) -> p hf d", hf=2)
                    P.dma("pool", dstq, srcq, writes=[r_qkv], semkey=r_qkv)
            P.dma("pool", t_qkv[:, c, 1024:1536], w_in["attn_wqkv"][c * 128:(c + 1) * 128, 1024:1536], writes=[r_qkv], semkey=r_qkv)
        t_awo, r_awo = scr["awo"]
        for a in range(2):
            for g in range(4):
                for half in range(2):
                    head = (2 * a + half) * 4 + g
                    P.dma("pool", t_awo[half * 64:(half + 1) * 64, a * 4 + g, :],
                          w_in["attn_wo"][head * 64:(head + 1) * 64, :], writes=[r_awo], semkey=r_awo)
        conv_w("xwq1", w_in["xattn_wq"][1], 8)
        conv_w("xwo1", w_in["xattn_wo"][1], 8)
        conv_w("ffn2_wi1", w_in["ffn2_wi"][1], 8)
        conv_w2("ffn2_wo1", w_in["ffn2_wo"][1], 22)

    cf = sb("cf", [128, NCF])
    cbf = sb("cbf", [128, 640], BF16)
    pf = sb("pf", [128, NPF])
    r_cf, r_cbf, r_pf = P.res("cf"), P.res("cbf"), P.res("pf")
    P.dma("sp", cf[:], cst_d, writes=[r_cf])
    P.dma("sp", pf[:], prm_d, writes=[r_pf])
    P.op("dve", lambda e: e.tensor_copy(cbf[:], cf[:, 0:640]), reads=[r_cf], writes=[r_cbf])
    CONST = [r_cf, r_cbf, r_pf]

    def cfs(name, lo=0, hi=None):
        o, w = CF[name]
        return cf[:, o + lo:o + (w if hi is None else hi)]

    def cbs(name, lo=0, hi=None):
        o, w = CF[name]
        return cbf[:, o + lo:o + (w if hi is None else hi)]

    def pfs(name, lo=0, hi=None):
        o, w = PF[name]
        return pf[:, o + lo:o + (w if hi is None else hi)]

    poolw = sb("poolw", [128, 4, 2, 256], BF16)
    r_poolw = P.res("poolw")
    def issue_poolw():
        for k in range(4):
            P.dma("pool", poolw[:, k, :, :], w_in["pool_w"][k].rearrange("(ic p) o -> p ic o", p=128),
                  writes=[r_poolw], semkey=r_poolw)
    CONST.append(r_poolw)

    dv = sb("dv", [128, 64])
    r_dv = P.res("dv")
    P.op("act", lambda e: e.activation(dv[:, 0:16], pfs("alog"), AF.Exp), reads=[r_pf], writes=[r_dv])
    P.op("act", lambda e: e.activation(dv[:, 16:24], pfs("sink"), AF.Exp), reads=[r_pf], writes=[r_dv])
    P.op("dve", lambda e: e.tensor_scalar(dv[:, 0:16], dv[:, 0:16], -1.0, None, ALU.mult), reads=[r_dv], writes=[r_dv])
    P.op("dve", lambda e: e.tensor_scalar(dv[:, 24:28], pfs("gxq"), 1.0 / 16.0, None, ALU.mult), reads=[r_pf, r_dv],
         writes=[r_dv])
    P.op("dve", lambda e: e.tensor_scalar(dv[:, 28:29], pfs("gq"), 1.0 / 8.0, None, ALU.mult), reads=[r_pf, r_dv],
         writes=[r_dv])
    CONST.append(r_dv)

    ps = nc.alloc_psum_tensor("ps", [128, 4096], F32)
    pres = [P.res(f"bank{i}") for i in range(8)]
    bptr = [0]

    def bank(n=1):
        b = bptr[0]
        if b % n:
            b += n - b % n
        if b + n > 8:
            b = 0
        bptr[0] = (b + n) % 8
        return ps[:, b * 512:(b + n) * 512], pres[b:b + n]

    NSLOT = 3
    SLOTE = 5632
    slots = [sb(f"wslot{i}", [128, SLOTE], BF16) for i in range(NSLOT)]
    sres = [P.res(f"wslot{i}") for i in range(NSLOT)]
    sptr = [0]

    def wload(srcs, name):
        i = sptr[0]
        sptr[0] = (i + 1) % NSLOT
        views = []
        off = 0
        for s in srcs:
            a, b = s.shape[1], s.shape[2]
            v = slots[i][:, off:off + a * b].rearrange("p (a b) -> p a b", a=a)
            P.dma("sp", v, s, reads=[scr[name][1]], writes=[sres[i]])
            views.append(v)
            off += a * b
        assert off <= SLOTE
        return views, sres[i]

    xT = sb("xT", [128, 8, TW])
    r_x = [P.res(f"x{c}") for c in range(8)]
    h = sb("h", [128, 8, TW], BF16)
    r_h = [P.res(f"h{c}") for c in range(8)]
    sq = sb("sq", [128, 8, TW], BF16)
    r_sq = [P.res(f"sq{c}") for c in range(8)]
    rstd = sb("rstd", [128, TW])
    r_rstd = P.res("rstd")
    rstd2 = sb("rstd2", [128, TW])
    r_rstd2 = P.res("rstd2")
    rsp = [0]
    act = sb("act", [128, 22, TW], BF16)
    r_act = [P.res(f"act{j}") for j in range(22)]
    xio = act[:].rearrange("p j t -> p (j t)").bitcast(F32)[:, 0:4096].rearrange("p (j d) -> p j d", j=4)
    r_sgt = [P.res(f"sgt{i}") for i in range(2)]
    sgp = [0]

    def mm(out, lhsT, rhs, start, stop, reads, writes):
        P.op("pe", lambda e: e.matmul(out, lhsT, rhs, start=start, stop=stop), reads=reads, writes=writes)

    def norm_fm(src, src_res, C, N, groups, lhsT, dim, gain_fn, out_fn, out_res, extra_reads=()):
        for c in range(C):
            rr = [src_res[c]] if isinstance(src_res, list) else [src_res]
            if c % 2 == 1:
                P.op("dve", lambda e, c=c: e.tensor_tensor(sq[:, c, 0:N], src(c), src(c), ALU.mult), reads=rr, writes=[r_sq[c]])
            else:
                P.op("act", lambda e, c=c: e.activation(sq[:, c, 0:N], src(c), AF.Square), reads=rr, writes=[r_sq[c]])
        for g in groups:
            bk, br = bank()
            rsp[0] ^= 1
            rs_, r_rs = (rstd, r_rstd) if rsp[0] else (rstd2, r_rstd2)
            for i, c in enumerate(g):
                mm(bk[:, 0:N], lhsT, sq[:, c, 0:N], i == 0, i == len(g) - 1, [r_sq[c], r_cbf], br)
            P.op("act", lambda e, bk=bk, rs_=rs_: e.activation(rs_[:, 0:N], bk[:, 0:N], AF.Ln, bias=EPS, scale=1.0 / dim),
                 reads=br, writes=[r_rs])
            P.op("act", lambda e, rs_=rs_: e.activation(rs_[:, 0:N], rs_[:, 0:N], AF.Exp, scale=-0.5),
                 reads=[r_rs], writes=[r_rs])
            for c in g:
                P.op("dve", lambda e, c=c, rs_=rs_: e.scalar_tensor_tensor(out_fn(c), src(c), gain_fn(c), rs_[:, 0:N],
                                                                           ALU.mult, ALU.mult),
                     reads=[r_rs, r_pf, r_dv] + ([src_res[c]] if isinstance(src_res, list) else [src_res]) + list(extra_reads),
                     writes=[out_res[c]] if isinstance(out_res, list) else [out_res])

    def norm_x(gname, l):
        o = l * 8
        norm_fm(lambda c: xT[:, c, :], r_x, 8, TW, [list(range(8))], cbs("ones"), float(D),
                lambda c: pfs(gname, o + c, o + c + 1), lambda c: h[:, c, :], r_h)

    def ffn(l, f):
        phase_barrier()
        apos[0] = 0
        sgt = [carve(TW, F32) for _ in range(2)]
        norm_x("gffn1" if f == "ffn1" else "gffn2", l)
        wi_s = scr[f"{f}_wi{l}"][0]
        wo_s = scr[f"{f}_wo{l}"][0]
        for (j0, n) in [(0, 4), (4, 4), (8, 4), (12, 4), (16, 4), (20, 2)]:
            (gv,), gr = wload([wi_s[:, :, j0 * 128:(j0 + n) * 128]], f"{f}_wi{l}")
            (uv,), ur = wload([wi_s[:, :, DFF + j0 * 128:DFF + (j0 + n) * 128]], f"{f}_wi{l}")
            for jj in range(n):
                j = j0 + jj
                bg, rg = bank()
                bu, ru = bank()
                for c in range(8):
                    mm(bg, gv[:, c, jj * 128:(jj + 1) * 128], h[:, c, :], c == 0, c == 7, [gr, r_h[c]], rg)
                for c in range(8):
                    mm(bu, uv[:, c, jj * 128:(jj + 1) * 128], h[:, c, :], c == 0, c == 7, [ur, r_h[c]], ru)
                si = sgp[0]
                sgp[0] ^= 1
                P.op("act", lambda e, bg=bg, si=si: e.activation(sgt[si], bg, AF.Silu), reads=rg, writes=[r_sgt[si]])
                P.op("dve", lambda e, bu=bu, si=si, j=j: e.tensor_tensor(act[:, j, :], sgt[si], bu, ALU.mult),
                     reads=ru + [r_sgt[si]], writes=[r_act[j]])
        for m0 in range(0, 8, 2):
            (wv,), wr = wload([wo_s[:, m0:m0 + 2, :]], f"{f}_wo{l}")
            for mi in range(2):
                m = m0 + mi
                bo, ro = bank()
                for j in range(22):
                    mm(bo, wv[:, mi, j * 128:(j + 1) * 128], act[:, j, :], j == 0, j == 21, [wr, r_act[j]], ro)
                P.op("dve", lambda e, bo=bo, m=m: e.scalar_tensor_tensor(xT[:, m, :], bo, 0.5, xT[:, m, :], ALU.mult, ALU.add),
                     reads=ro + [r_x[m]], writes=[r_x[m]])

    def load_x(ti):
        P.dma("sp", xio, x_d[ti * TW:(ti + 1) * TW, :].rearrange("(j p) d -> p j d", p=128), reads=[], writes=r_act)
        for c in range(8):
            bk, br = bank()
            for j in range(4):
                P.op("pe", lambda e, bk=bk, c=c, j=j: e.transpose(bk[:, j * 128:(j + 1) * 128], xio[:, j, c * 128:(c + 1) * 128],
                                                                 cfs("ident")), reads=r_act + [r_cf], writes=br)
            eng = "act" if c % 2 else "dve"
            if eng == "act":
                P.op("act", lambda e, bk=bk, c=c: e.copy(xT[:, c, :], bk), reads=br, writes=[r_x[c]])
            else:
                P.op("dve", lambda e, bk=bk, c=c: e.tensor_copy(xT[:, c, :], bk), reads=br, writes=[r_x[c]])

    def store_x(ti, to):
        for j in range(4):
            for hf in range(2):
                bk, br = bank()
                for cc in range(4):
                    c = hf * 4 + cc
                    P.op("pe", lambda e, bk=bk, c=c, cc=cc, j=j: e.transpose(bk[:, cc * 128:(cc + 1) * 128],
                                                                            xT[:, c, j * 128:(j + 1) * 128], cfs("ident")),
                         reads=[r_x[c], r_cf], writes=br)
                if hf:
                    P.op("act", lambda e, bk=bk, j=j, hf=hf: e.copy(xio[:, j, hf * 512:(hf + 1) * 512], bk), reads=br,
                         writes=r_act)
                else:
                    P.op("dve", lambda e, bk=bk, j=j, hf=hf: e.tensor_copy(xio[:, j, hf * 512:(hf + 1) * 512], bk), reads=br,
                         writes=r_act)
        P.dma("sp", out_d[to * TW:(to + 1) * TW, :].rearrange("(j p) d -> p j d", p=128), xio, reads=r_act, writes=[],
              semkey="out", final=True)

    r_arena = P.res("arena")
    P.auto = r_arena

    def phase_barrier():
        P.op("dve", lambda e: e.memset(rstd[:, 0:1], 0.0), reads=[], writes=[r_arena, r_rstd])

    A0 = nc.alloc_sbuf_tensor("arena", [128, 36 * 1024], BF16)
    apos = [0]

    A1 = act[:].rearrange("p j t -> p (j t)")
    apos1 = [0]

    def carve(n_el, dt, reg=0):
        k = 2 if dt == F32 else 1
        pos, A, cap = (apos, A0, 36 * 1024) if reg == 0 else (apos1, A1, 22 * TW)
        o = pos[0]
        o = (o + 15) // 16 * 16
        pos[0] = o + n_el * k
        assert pos[0] <= cap, (reg, pos[0])
        v = A[:, o:o + n_el * k]
        return v.bitcast(F32) if dt == F32 else v

    memk = sb("memk", [128, 8, 256], BF16)
    memv = sb("memv", [128, 2, D], BF16)
    r_memk, r_memv = P.res("memk"), P.res("memv")

    def mem_kv():
        phase_barrier()
        apos[0] = 0
        memtm = xio[:, 0:2, :]
        P.dma("sp", memtm, mem_d.rearrange("(j p) d -> p j d", p=128), writes=r_act)
        mfm = carve(8 * 256, F32).rearrange("p (c m) -> p c m", c=8)
        for c in range(8):
            bk, br = bank()
            for j in range(2):
                P.op("pe", lambda e, bk=bk, c=c, j=j: e.transpose(bk[:, j * 128:(j + 1) * 128], xio[:, j, c * 128:(c + 1) * 128],
                                                                 cfs("ident")), reads=r_act + [r_cf], writes=br)
            P.op("dve", lambda e, bk=bk, c=c: e.tensor_copy(mfm[:, c, :], bk[:, 0:256]), reads=br + [r_arena], writes=[r_arena])
        hm = carve(8 * 256, BF16).rearrange("p (c m) -> p c m", c=8)
        norm_fm(lambda c: mfm[:, c, :], r_arena, 8, 256, [list(range(8))], cbs("ones"), float(D),
                lambda c: pfs("gmem", c, c + 1), lambda c: hm[:, c, :], r_arena)
        kf = carve(8 * 256, F32).rearrange("p (c m) -> p c m", c=8)
        wk = scr["wkv"][0]
        for q4 in range(4):
            (wv,), wr = wload([wk[:, :, q4 * 512:(q4 + 1) * 512]], "wkv")
            if q4 < 2:
                for oc in range(4):
                    ch = q4 * 4 + oc
                    bk, br = bank()
                    for c in range(8):
                        mm(bk[:, 0:256], wv[:, c, oc * 128:(oc + 1) * 128], hm[:, c, :], c == 0, c == 7, [wr, r_arena], br)
                    P.op("act", lambda e, bk=bk, ch=ch: e.copy(kf[:, ch, :], bk[:, 0:256]), reads=br + [r_arena], writes=[r_arena])
            else:
                for mc in range(2):
                    bk, br = bank()
                    for c in range(8):
                        mm(bk, hm[:, c, mc * 128:(mc + 1) * 128], wv[:, c, :], c == 0, c == 7, [wr, r_arena], br)
                    P.op("act", lambda e, bk=bk, mc=mc, q4=q4: e.copy(memv[:, mc, (q4 - 2) * 512:(q4 - 1) * 512], bk),
                         reads=br, writes=[r_memv])
        norm_fm(lambda c: kf[:, c, :], r_arena, 8, 256, [[2 * hh, 2 * hh + 1] for hh in range(4)], cbs("ones"), 256.0,
                lambda c: pfs("gmemk", c % 2, c % 2 + 1), lambda c: memk[:, c, :], r_memk)

    def xattn(l):
        phase_barrier()
        apos[0] = 0
        norm_x("gxat", l)
        qf = carve(8 * TW, F32).rearrange("p (c t) -> p c t", c=8)
        qn = carve(8 * TW, BF16).rearrange("p (c t) -> p c t", c=8)
        of = carve(8 * TW, BF16).rearrange("p (c t) -> p c t", c=8)
        eT = carve(2 * 2 * TW, BF16).rearrange("p (i m t) -> p i m t", i=2, m=2)
        rden = carve(2 * TW, F32).rearrange("p (i t) -> p i t", i=2)
        r_qf = [P.res() for _ in range(8)]
        r_qn = [P.res() for _ in range(8)]
        r_of = [P.res() for _ in range(8)]
        r_eT = [P.res() for _ in range(2)]
        r_rden = [P.res() for _ in range(2)]
        wq = scr[f"xwq{l}"][0]
        wo = scr[f"xwo{l}"][0]
        for q2 in range(2):
            (wv,), wr = wload([wq[:, :, q2 * 512:(q2 + 1) * 512]], f"xwq{l}")
            for oc in range(4):
                ch = q2 * 4 + oc
                bk, br = bank()
                for c in range(8):
                    mm(bk, wv[:, c, oc * 128:(oc + 1) * 128], h[:, c, :], c == 0, c == 7, [wr, r_h[c], r_arena], br)
                P.op("act", lambda e, bk=bk, ch=ch: e.copy(qf[:, ch, :], bk), reads=br + [r_arena], writes=[r_qf[ch]])
        norm_fm(lambda c: qf[:, c, :], r_qf, 8, TW, [[2 * hh, 2 * hh + 1] for hh in range(4)], cbs("ones"), 256.0,
                lambda c: dv[:, 24 + 2 * l + c % 2:25 + 2 * l + c % 2], lambda c: qn[:, c, :], r_qn)
        for hh in range(4):
            i = hh % 2
            for mc in range(2):
                bk, br = bank()
                for cc in range(2):
                    mm(bk, memk[:, 2 * hh + cc, mc * 128:(mc + 1) * 128], qn[:, 2 * hh + cc, :], cc == 0, cc == 1,
                       [r_memk, r_qn[2 * hh + cc]], br)
                P.op("act", lambda e, bk=bk, i=i, mc=mc: e.activation(eT[:, i, mc, :], bk, AF.Exp), reads=br + [r_arena],
                     writes=[r_eT[i]])
            bk, br = bank()
            for mc in range(2):
                mm(bk, cbs("ones"), eT[:, i, mc, :], mc == 0, mc == 1, [r_cbf, r_eT[i]], br)
            P.op("act", lambda e, bk=bk, i=i: e.activation(rden[:, i, :], bk, AF.Ln), reads=br + [r_arena], writes=[r_rden[i]])
            P.op("act", lambda e, i=i: e.activation(rden[:, i, :], rden[:, i, :], AF.Exp, scale=-1.0), reads=[r_rden[i]],
                 writes=[r_rden[i]])
            for dc in range(2):
                bk, br = bank()
                for mc in range(2):
                    mm(bk, memv[:, mc, hh * 256 + dc * 128:hh * 256 + (dc + 1) * 128], eT[:, i, mc, :], mc == 0, mc == 1,
                       [r_memv, r_eT[i]], br)
                P.op("dve", lambda e, bk=bk, i=i, c=2 * hh + dc: e.tensor_tensor(of[:, c, :], bk, rden[:, i, :], ALU.mult),
                     reads=br + [r_rden[i], r_arena], writes=[r_of[2 * hh + dc]])
        for q2 in range(2):
            (wv,), wr = wload([wo[:, :, q2 * 512:(q2 + 1) * 512]], f"xwo{l}")
            for oc in range(4):
                m = q2 * 4 + oc
                bk, br = bank()
                for c in range(8):
                    mm(bk, wv[:, c, oc * 128:(oc + 1) * 128], of[:, c, :], c == 0, c == 7, [wr, r_of[c]], br)
                P.op("dve", lambda e, bk=bk, m=m: e.tensor_tensor(xT[:, m, :], bk, xT[:, m, :], ALU.add),
                     reads=br + [r_x[m]], writes=[r_x[m]])

    Hst = sb("Hst", [128, D])
    Hb = sb("Hb", [128, D], BF16)
    r_H, r_Hb = P.res("H"), P.res("Hb")
    cpre = sb("cpre", [128, 12, 4])
    r_cpre = P.res("cpre")
    phalo = sb("phalo", [128, 8, 16])
    r_phalo = P.res("phalo")
    kbuf = sb("kbuf", [128, 2, 128 + TW], BF16)
    r_kbuf = P.res("kbuf")
    vbuf = sb("vbuf", [128, 5, 2, 2, 128], BF16)
    r_vbuf = P.res("vbuf")
    P.op("pool", lambda e: e.memset(Hst[:], 0.0), writes=[r_H])
    P.op("pool", lambda e: e.memset(Hb[:], 0.0), writes=[r_Hb])
    P.op("pool", lambda e: e.memset(cpre[:], 0.0), writes=[r_cpre])
    P.op("pool", lambda e: e.memset(phalo[:], 0.0), writes=[r_phalo])
    P.op("pool", lambda e: e.memset(kbuf[:], 0.0), writes=[r_kbuf])
    P.op("pool", lambda e: e.memset(vbuf[:], 0.0), writes=[r_vbuf])

    def mixer0(ti, first, mode="full"):
        full = mode == "full"
        phase_barrier()
        apos[0] = 0
        norm_x("gmix", 0)
        W = scr["inproj"][0]
        xbc = carve(12 * TW, BF16).rearrange("p (c t) -> p c t", c=12)
        pooled = carve(8 * TW, BF16).rearrange("p (c t) -> p c t", c=8)
        apos1[0] = 0
        mixin = carve(16 * TW, BF16, 1).rearrange("p (c t) -> p c t", c=16)
        szb = carve(4 * D, BF16).rearrange("p (j d) -> p j d", j=4)
        dtt = carve(4 * 16, F32).rearrange("p (j d) -> p j d", j=4)
        tcv = [carve(516, F32) for _ in range(2)]
        tacc = [carve(TW, F32) for _ in range(2)]
        tu = [carve(528, F32) for _ in range(3)]
        r_xbc = [P.res() for _ in range(12)]
        r_pooled = [P.res() for _ in range(8)]
        r_mixin = [P.res() for _ in range(16)]
        r_sz = [P.res() for _ in range(4)]
        r_dt = [P.res() for _ in range(4)]
        r_tcv = [P.res() for _ in range(2)]
        r_tacc = [P.res() for _ in range(2)]
        r_tu = [P.res() for _ in range(3)]
        AR = [r_arena]

        if full:
            (wz0,), wzr0 = wload([W[:, :, 0:512]], "inproj")
            (wz1,), wzr1 = wload([W[:, :, 512:1024]], "inproj")
            for j in range(4):
                bk, br = bank(2)
                for hf, (wv, wr) in enumerate([(wz0, wzr0), (wz1, wzr1)]):
                    for c in range(8):
                        mm(bk[:, hf * 512:(hf + 1) * 512], h[:, c, j * 128:(j + 1) * 128], wv[:, c, :], c == 0, c == 7,
                           [wr, r_h[c]] + AR, [br[hf]])
                P.op("act", lambda e, bk=bk, j=j: e.activation(szb[:, j, :], bk, AF.Silu), reads=br + AR, writes=[r_sz[j]])
        (wdt,), wdtr = wload([W[:, :, 2560:2576]], "inproj")
        for j in range(4):
            bk, br = bank()
            for c in range(8):
                mm(bk[:, 0:16], h[:, c, j * 128:(j + 1) * 128], wdt[:, c, :], c == 0, c == 7, [wdtr, r_h[c]] + AR, br)
            P.op("dve", lambda e, bk=bk, j=j: e.tensor_tensor(dtt[:, j, :], bk[:, 0:16], pfs("dtb"), ALU.add),
                 reads=br + [r_pf] + AR, writes=[r_dt[j]])
            P.op("act", lambda e, j=j: e.activation(dtt[:, j, :], dtt[:, j, :], AF.Exp), reads=[r_dt[j]], writes=[r_dt[j]])
            P.op("act", lambda e, j=j: e.activation(dtt[:, j, :], dtt[:, j, :], AF.Ln, bias=1.0), reads=[r_dt[j]],
                 writes=[r_dt[j]])

        for g3 in range(3):
            (wv,), wr = wload([W[:, :, 1024 + g3 * 512:1024 + (g3 + 1) * 512]], "inproj")
            for oc in range(4):
                ch = g3 * 4 + oc
                bk, br = bank()
                for c in range(8):
                    mm(bk, wv[:, c, oc * 128:(oc + 1) * 128], h[:, c, :], c == 0, c == 7, [wr, r_h[c]] + AR, br)
                i = ch % 2
                P.op("act", lambda e, i=i, ch=ch: e.copy(tcv[i][:, 0:3], cpre[:, ch, 0:3]), reads=[r_cpre] + AR,
                     writes=[r_tcv[i]])
                P.op("act", lambda e, bk=bk, i=i: e.copy(tcv[i][:, 3:515], bk), reads=br + [r_tcv[i]], writes=[r_tcv[i]])
                P.op("dve", lambda e, i=i, ch=ch: e.tensor_copy(cpre[:, ch, 0:3], tcv[i][:, 512:515]), reads=[r_tcv[i]],
                     writes=[r_cpre])
                P.op("dve", lambda e, i=i, ch=ch: e.tensor_scalar(tacc[i][:], tcv[i][:, 0:512], pfs("convw", ch * 4, ch * 4 + 1),
                                                                  None, ALU.mult),
                     reads=[r_tcv[i], r_pf] + AR, writes=[r_tacc[i]])
                for tp in range(1, 4):
                    P.op("dve", lambda e, i=i, ch=ch, tp=tp: e.scalar_tensor_tensor(
                        tacc[i][:], tcv[i][:, tp:tp + 512], pfs("convw", ch * 4 + tp, ch * 4 + tp + 1), tacc[i][:],
                        ALU.mult, ALU.add), reads=[r_tcv[i], r_tacc[i], r_pf], writes=[r_tacc[i]])
                P.op("act", lambda e, i=i, ch=ch: e.activation(xbc[:, ch, :], tacc[i][:], AF.Silu,
                                                               bias=pfs("convb", ch, ch + 1)),
                     reads=[r_tacc[i], r_pf] + AR, writes=[r_xbc[ch]])

        for g2 in (range(2) if mode != "lite" else []):
            (wv,), wr = wload([W[:, :, 2576 + g2 * 512:2576 + (g2 + 1) * 512]], "inproj")
            for oc in range(4):
                ch = g2 * 4 + oc
                k = ch // 2
                w = (2, 4, 8, 16)[k]
                bk, br = bank()
                for c in range(8):
                    mm(bk, wv[:, c, oc * 128:(oc + 1) * 128], h[:, c, :], c == 0, c == 7, [wr, r_h[c]] + AR, br)
                t0, t1, t2 = tu
                P.op("pool", lambda e, ch=ch: e.tensor_copy(tu[0][:, 0:16], phalo[:, ch, :]), reads=[r_phalo] + AR,
                     writes=[r_tu[0]])
                P.op("act", lambda e, bk=bk: e.copy(tu[0][:, 16:528], bk), reads=br + [r_tu[0]], writes=[r_tu[0]])
                P.op("pool", lambda e, ch=ch: e.tensor_copy(phalo[:, ch, :], tu[0][:, 512:528]), reads=[r_tu[0]],
                     writes=[r_phalo])
                if not full:
                    continue
                src, si = tu[0], 0
                sh = 1
                lo = 0
                while sh < w:
                    di = 1 if si != 1 else 2
                    dst = tu[di]
                    lo2 = lo + sh
                    P.op("pool", lambda e, dst=dst, src=src, lo2=lo2, sh=sh: e.tensor_tensor(
                        dst[:, lo2:528], src[:, lo2:528], src[:, lo2 - sh:528 - sh], ALU.add),
                        reads=[r_tu[si]] + AR, writes=[r_tu[di]])
                    src, si, lo = dst, di, lo2
                    sh *= 2
                P.op("dve", lambda e, src=src, ch=ch, w=w: e.scalar_tensor_tensor(
                    pooled[:, ch, :], src[:, 16:528], 1.0 / w, tu[0][:, 16:528], ALU.mult, ALU.subtract),
                    reads=[r_tu[si], r_tu[0]] + AR, writes=[r_pooled[ch]])
                if first:
                    ri = 3 - si
                    tmp = tu[ri]
                    P.op("pool", lambda e, src=src, tmp=tmp, k=k: e.tensor_tensor(
                        tmp[:, 0:16], src[:, 16:32], cfs("invc", k * 16, k * 16 + 16), ALU.mult),
                        reads=[r_tu[si], r_cf] + AR, writes=[r_tu[ri]])
                    P.op("pool", lambda e, tmp=tmp, ch=ch: e.tensor_tensor(
                        pooled[:, ch, 0:16], tmp[:, 0:16], tu[0][:, 16:32], ALU.subtract),
                        reads=[r_tu[ri], r_tu[0], r_pooled[ch]], writes=[r_pooled[ch]])
        for k in (range(4) if full else []):
            for oc in range(2):
                bk, br = bank()
                for ic in range(2):
                    mm(bk, poolw[:, k, ic, oc * 128:(oc + 1) * 128], pooled[:, 2 * k + ic, :], ic == 0, ic == 1,
                       [r_poolw, r_pooled[2 * k + ic]], br)
                ch = 2 * k + oc
                P.op("act", lambda e, bk=bk, ch=ch: e.activation(mixin[:, 8 + ch, :], bk, AF.Copy,
                                                                 scale=pfs("pscale", ch, ch + 1)),
                     reads=br + [r_pf] + AR, writes=[r_mixin[8 + ch]])

        xr = carve(D, BF16).rearrange("p (h d) -> p h d", h=16)
        xrd = carve(D, BF16).rearrange("p (h d) -> p h d", h=16)
        xtm = carve(D, F32)
        Btm = carve(256, BF16)
        sm = carve(128, F32)
        cbT = carve(256, F32).rearrange("p (g l) -> p g l", g=2)
        Rm = carve(8 * 128, F32).rearrange("p (h l) -> p h l", h=8)
        Em = carve(8 * 128, F32).rearrange("p (h l) -> p h l", h=8)
        MT = carve(16 * 128, BF16).rearrange("p (h l) -> p h l", h=16)
        t1 = carve(D, F32)
        t2 = carve(D, F32, 1)
        ytm = carve(D, BF16)
        ss = carve(8, F32)
        r_xr, r_xrd, r_xtm, r_Btm, r_sm, r_cbT, r_Rm, r_Em, r_MT, r_t1, r_t2, r_ytm, r_ss = [P.res() for _ in range(13)]
        a_neg = dv[:, 0:16]
        for j in range(4):
            tsl = slice(j * 128, (j + 1) * 128)
            bx, rx = bank(2)
            bxb = bx.bitcast(BF16)
            for c in range(8):
                P.op("pe", lambda e, c=c, tsl=tsl, bxb=bxb: e.transpose(bxb[:, c * 128:(c + 1) * 128], xbc[:, c, tsl], cbs("ident")),
                     reads=[r_xbc[c], r_cbf] + AR, writes=rx)
            for g in range(2):
                P.op("pe", lambda e, g=g, tsl=tsl, bxb=bxb: e.transpose(bxb[:, 1024 + g * 128:1024 + (g + 1) * 128],
                                                                        xbc[:, 8 + g, tsl], cbs("ident")),
                     reads=[r_xbc[8 + g], r_cbf] + AR, writes=rx)
            P.op("act", lambda e, bxb=bxb: e.copy(xtm[:], bxb[:, 0:1024]), reads=rx + AR, writes=[r_xtm])
            P.op("act", lambda e, bxb=bxb: e.copy(Btm[:], bxb[:, 1024:1280]), reads=rx + AR, writes=[r_Btm])
            P.op("dve", lambda e, j=j: e.tensor_tensor(sm[:, 80:96], dtt[:, j, :], a_neg, ALU.mult),
                 reads=[r_dt[j], r_dv] + AR, writes=[r_sm])
            bs, rs = bank()
            mm(bs[:, 0:16], cfs("tri"), sm[:, 80:96], True, True, [r_cf, r_sm], rs)
            mm(bs[:, 16:32], cfs("ones"), sm[:, 80:96], True, True, [r_cf, r_sm], rs)
            P.op("dve", lambda e, bs=bs: e.tensor_copy(sm[:, 0:32], bs[:, 0:32]), reads=rs + [r_sm], writes=[r_sm])
            P.op("dve", lambda e: e.tensor_tensor(sm[:, 32:48], sm[:, 16:32], sm[:, 0:16], ALU.subtract), reads=[r_sm],
                 writes=[r_sm])
            P.op("act", lambda e: e.activation(sm[:, 32:48], sm[:, 32:48], AF.Exp), reads=[r_sm], writes=[r_sm])
            P.op("act", lambda e: e.activation(sm[:, 48:64], sm[:, 0:16], AF.Exp), reads=[r_sm], writes=[r_sm])
            P.op("act", lambda e: e.activation(sm[:, 64:80], sm[:, 16:32], AF.Exp), reads=[r_sm], writes=[r_sm])
            xtm3 = xtm.rearrange("p (h d) -> p h d", h=16)
            P.op("dve", lambda e, j=j, xtm3=xtm3: e.tensor_tensor(xr[:], xtm3, dtt[:, j, :].unsqueeze(2).broadcast_to([128, 16, 64]),
                                                                  ALU.mult), reads=[r_xtm, r_dt[j]] + AR, writes=[r_xr])
            P.op("dve", lambda e: e.tensor_tensor(xrd[:], xr[:], sm[:, 32:48].unsqueeze(2).broadcast_to([128, 16, 64]), ALU.mult),
                 reads=[r_xr, r_sm], writes=[r_xrd])
            def state_update():
                bS, rS = bank(2)
                for g in range(2):
                    mm(bS[:, g * 512:(g + 1) * 512], Btm[:, g * 128:(g + 1) * 128], xrd[:, g * 8:(g + 1) * 8, :], True, True,
                       [r_Btm, r_xrd], [rS[g]])
                P.op("dve", lambda e: e.tensor_tensor(Hst[:].rearrange("p (h d) -> p h d", h=16),
                                                      Hst[:].rearrange("p (h d) -> p h d", h=16),
                                                      sm[:, 64:80].unsqueeze(2).broadcast_to([128, 16, 64]), ALU.mult),
                     reads=[r_H, r_sm], writes=[r_H])
                P.op("dve", lambda e, bS=bS: e.tensor_tensor(Hst[:], Hst[:], bS, ALU.add), reads=rS + [r_H], writes=[r_H])
                P.op("act", lambda e: e.copy(Hb[:], Hst[:]), reads=[r_H], writes=[r_Hb])

            if full:
                bc, rc = bank()
                for g in range(2):
                    mm(bc[:, g * 128:(g + 1) * 128], xbc[:, 8 + g, tsl], xbc[:, 10 + g, tsl], True, True,
                       [r_xbc[8 + g], r_xbc[10 + g]] + AR, rc)
                P.op("dve", lambda e, bc=bc: e.tensor_tensor(cbT[:], bc[:, 0:256].rearrange("p (g l) -> p g l", g=2),
                                                             cfs("tri").unsqueeze(1).broadcast_to([128, 2, 128]), ALU.mult),
                     reads=rc + [r_cf] + AR, writes=[r_cbT])
                bo_, ro_ = bank(2)
                for g in range(2):
                    mm(bo_[:, g * 512:(g + 1) * 512], xbc[:, 10 + g, tsl], Hb[:, g * 512:(g + 1) * 512], True, True,
                       [r_xbc[10 + g], r_Hb] + AR, [ro_[g]])
                P.op("dve", lambda e, bo_=bo_: e.tensor_tensor(t1.rearrange("p (h d) -> p h d", h=16),
                                                               bo_.rearrange("p (h d) -> p h d", h=16),
                                                               sm[:, 48:64].unsqueeze(2).broadcast_to([128, 16, 64]), ALU.mult),
                     reads=ro_ + [r_sm] + AR, writes=[r_t1])
                state_update()
                for g in range(2):
                    P.op("dve", lambda e, g=g: e.tensor_tensor(Rm[:], cfs("tri").unsqueeze(1).broadcast_to([128, 8, 128]),
                                                               sm[:, 80 + g * 8:88 + g * 8].unsqueeze(2).broadcast_to([128, 8, 128]),
                                                               ALU.mult), reads=[r_cf, r_sm] + AR, writes=[r_Rm])
                    bd, rd = bank(2)
                    for hf in range(2):
                        mm(bd[:, hf * 512:(hf + 1) * 512], cfs("ustr"), Rm[:, hf * 4:(hf + 1) * 4, :], True, True, [r_cf, r_Rm], [rd[hf]])
                    P.op("act", lambda e, bd=bd: e.activation(Em.rearrange("p h l -> p (h l)"), bd, AF.Exp), reads=rd + AR,
                         writes=[r_Em])
                    P.op("dve", lambda e, g=g: e.tensor_tensor(MT[:, g * 8:(g + 1) * 8, :], Em[:],
                                                               cbT[:, g, :].unsqueeze(1).broadcast_to([128, 8, 128]), ALU.mult),
                         reads=[r_Em, r_cbT], writes=[r_MT])
                by, ry = bank(2)
                for hh in range(16):
                    mm(by[:, hh * 64:(hh + 1) * 64], MT[:, hh, :], xr[:, hh, :], True, True, [r_MT, r_xr], [ry[hh // 8]])
                P.op("dve", lambda e, by=by: e.tensor_tensor(t1[:], t1[:], by, ALU.add), reads=ry + [r_t1], writes=[r_t1])
                P.op("pool", lambda e: e.tensor_tensor(t2[:], xtm[:], pfs("dexp"), ALU.mult), reads=[r_xtm, r_pf] + AR, writes=[r_t2])
                P.op("dve", lambda e: e.tensor_tensor(t1[:], t1[:], t2[:], ALU.add), reads=[r_t1, r_t2], writes=[r_t1])
                P.op("dve", lambda e, j=j: e.tensor_tensor(t1[:], t1[:], szb[:, j, :], ALU.mult), reads=[r_t1, r_sz[j]], writes=[r_t1])
                P.op("dve", lambda e: e.memset(ss[:, 0:2], 0.0), reads=[r_ss], writes=[r_ss])
                for g in range(2):
                    P.op("act", lambda e, g=g: e.activation(t2[:, g * 512:(g + 1) * 512], t1[:, g * 512:(g + 1) * 512], AF.Square,
                                                            accum_out=ss[:, g:g + 1]), reads=[r_t1, r_t2], writes=[r_t2, r_ss])
                P.op("act", lambda e: e.activation(ss[:, 2:4], ss[:, 0:2], AF.Ln, bias=EPS, scale=1.0 / 512.0), reads=[r_ss], writes=[r_ss])
                P.op("act", lambda e: e.activation(ss[:, 2:4], ss[:, 2:4], AF.Exp, scale=-0.5), reads=[r_ss], writes=[r_ss])
                for g in range(2):
                    P.op("dve", lambda e, g=g: e.scalar_tensor_tensor(ytm[:, g * 512:(g + 1) * 512], t1[:, g * 512:(g + 1) * 512],
                                                                      ss[:, 2 + g:3 + g], pfs("ssdn", g * 512, (g + 1) * 512),
                                                                      ALU.mult, ALU.mult),
                         reads=[r_t1, r_ss, r_pf], writes=[r_ytm])
                bt, rt = bank()
                btb = bt.bitcast(BF16)
                for c in range(8):
                    P.op("pe", lambda e, c=c, btb=btb: e.transpose(btb[:, c * 128:(c + 1) * 128], ytm[:, c * 128:(c + 1) * 128],
                                                                   cbs("ident")), reads=[r_ytm, r_cbf], writes=rt)
                P.op("act", lambda e, btb=btb, tsl=tsl: e.copy(mixin[:, 0:8, tsl], btb[:, 0:1024].rearrange("p (c t) -> p c t", c=8)),
                     reads=rt + AR, writes=[r_mixin[c] for c in range(8)])
            if not full:
                state_update()

        if full:
            Wo = scr["outproj"][0]
            for m0 in range(0, 8, 2):
                (wv,), wr = wload([Wo[:, m0:m0 + 2, :]], "outproj")
                for mi in range(2):
                    m = m0 + mi
                    bk, br = bank()
                    for c in range(16):
                        mm(bk, wv[:, mi, c * 128:(c + 1) * 128], mixin[:, c, :], c == 0, c == 15, [wr, r_mixin[c]], br)
                    P.op("dve", lambda e, bk=bk, m=m: e.tensor_tensor(xT[:, m, :], bk, xT[:, m, :], ALU.add),
                         reads=br + [r_x[m]], writes=[r_x[m]])

    QPERM = _qperm()

    def mixer1(ti, first, mode="full"):
        full = mode == "full"
        phase_barrier()
        apos[0] = 0
        norm_x("gmix", 1)
        AR = [r_arena]
        W = scr["wqkv"][0]
        qf = carve(8 * TW, F32).rearrange("p (c t) -> p c t", c=8)
        qn = carve(8 * TW, BF16).rearrange("p (c t) -> p c t", c=8)
        kf = carve(2 * TW, F32).rearrange("p (c t) -> p c t", c=2)
        of = carve(8 * TW, BF16).rearrange("p (c t) -> p c t", c=8)
        st = [carve(512, F32).rearrange("p (i k q) -> p i k q", i=2, k=2) for _ in range(2)]
        eT = [carve(512, BF16).rearrange("p (i k q) -> p i k q", i=2, k=2) for _ in range(2)]
        dn = carve(1024, F32).rearrange("p (c q) -> p c q", c=8)
        r_qf = [P.res() for _ in range(8)]
        r_qn = [P.res() for _ in range(8)]
        r_kf = [P.res() for _ in range(2)]
        r_of = [P.res() for _ in range(8)]
        r_st = [P.res() for _ in range(2)]
        r_eT = [P.res() for _ in range(2)]
        r_dn = P.res()
        if full:
            (w0,), wr0 = wload([W[:, :, 0:512]], "wqkv")
            (w1,), wr1 = wload([W[:, :, 512:1024]], "wqkv")
        (w2,), wr2 = wload([W[:, :, 1024:1536]], "wqkv")
        for a in (range(2) if full else []):
            for g in range(4):
                ch = a * 4 + g
                bk, br = bank()
                wv, wr = (w0, wr0) if a == 0 else (w1, wr1)
                for c in range(8):
                    mm(bk, wv[:, c, g * 128:(g + 1) * 128], h[:, c, :], c == 0, c == 7, [wr, r_h[c]], br)
                P.op("act", lambda e, bk=bk, ch=ch: e.activation(qf[:, ch, :], bk, AF.Identity, bias=pfs("bq", ch, ch + 1)),
                     reads=br + [r_pf] + AR, writes=[r_qf[ch]])
        for kc in range(2):
            bk, br = bank()
            for c in range(8):
                mm(bk, w2[:, c, kc * 128:(kc + 1) * 128], h[:, c, :], c == 0, c == 7, [wr2, r_h[c]] + AR, br)
            P.op("act", lambda e, bk=bk, kc=kc: e.activation(kf[:, kc, :], bk, AF.Identity, bias=pfs("bk", kc, kc + 1)),
                 reads=br + [r_pf] + AR, writes=[r_kf[kc]])
        for j in range(4):
            bk, br = bank()
            for c in range(8):
                mm(bk[:, 0:256], h[:, c, j * 128:(j + 1) * 128], w2[:, c, 256:512], c == 0, c == 7, [wr2, r_h[c]] + AR, br)
            for a in range(2):
                for half in range(2):
                    kv = 2 * a + half
                    P.op("dve", lambda e, bk=bk, j=j, a=a, half=half, kv=kv: e.tensor_tensor(
                        vbuf[:, 1 + j, a, half, half * 64:(half + 1) * 64], bk[:, kv * 64:(kv + 1) * 64],
                        pfs("bv", kv * 64, (kv + 1) * 64), ALU.add), reads=br + [r_pf, r_vbuf], writes=[r_vbuf])
        if full:
            norm_fm(lambda c: qf[:, c, :], r_qf, 8, TW, [[c] for c in range(8)], cbs("bones"), 64.0,
                    lambda c: dv[:, 28:29], lambda c: qn[:, c, :], r_qn)
        norm_fm(lambda c: kf[:, c, :], r_kf, 2, TW, [[c] for c in range(2)], cbs("bones"), 64.0,
                lambda c: pfs("gk"), lambda c: kbuf[:, c, 128:128 + TW], r_kbuf, extra_reads=[r_kbuf])
        for j in (range(4) if full else []):
            qsl = slice(j * 128, (j + 1) * 128)
            nd = cfs("negd1") if (first and j == 0) else cfs("negd")
            bo_, ro_ = ps[:, 0:1024], pres[0:2]
            bd_, rd_ = ps[:, 1024:2048], pres[2:4]
            for a in range(2):
                for g in range(4):
                    ch = a * 4 + g
                    i = ch % 2
                    bsh = [ps[:, (4 + 2 * i + hf) * 512:(5 + 2 * i + hf) * 512] for hf in range(2)]
                    rsh = [[pres[4 + 2 * i + hf]] for hf in range(2)]
                    for half in range(2):
                        for kb in range(2):
                            P.op("pe", lambda e, half=half, kb=kb, a=a, ch=ch, j=j, qsl=qsl, bsx=bsh[half]: e.matmul(
                                bsx[:, kb * 128:(kb + 1) * 128],
                                kbuf[half * 64:(half + 1) * 64, a, (j + kb) * 128:(j + kb + 1) * 128],
                                qn[half * 64:(half + 1) * 64, ch, qsl], start=True, stop=True),
                                reads=[r_kbuf, r_qn[ch]] + AR, writes=rsh[half])
                    for half in range(2):
                        head = (2 * a + half) * 4 + g
                        P.op("dve", lambda e, i=i, half=half, head=head, nd=nd, bsx=bsh[half]: e.scalar_tensor_tensor(
                            st[i][:, half, :, :], nd.rearrange("p (k q) -> p k q", k=2), float(SLOPES[head]),
                            bsx[:, 0:256].rearrange("p (k q) -> p k q", k=2), ALU.mult, ALU.add),
                            reads=rsh[half] + [r_cf] + AR, writes=[r_st[i]])
                    P.op("act", lambda e, i=i: e.activation(eT[i][:], st[i][:], AF.Exp), reads=[r_st[i]] + AR, writes=[r_eT[i]])
                    n = 0
                    for half in range(2):
                        for kb in range(2):
                            mm(bo_[:, ch * 128:(ch + 1) * 128], vbuf[:, j + kb, a, half, :], eT[i][:, half, kb, :], n == 0, n == 3,
                               [r_vbuf, r_eT[i]], [ro_[ch // 4]])
                            n += 1
                    n = 0
                    for half in range(2):
                        for kb in range(2):
                            mm(bd_[:, ch * 128:(ch + 1) * 128], cbs("oh1") if half else cbs("oh0"), eT[i][:, half, kb, :],
                               n == 0, n == 3, [r_cbf, r_eT[i]], [rd_[ch // 4]])
                            n += 1
            P.op("dve", lambda e, bd_=bd_: e.tensor_tensor(dn[:], bd_.rearrange("p (c q) -> p c q", c=8),
                                                           dv[:, 16:24].unsqueeze(2).broadcast_to([128, 8, 128]), ALU.add),
                 reads=rd_ + [r_dv] + AR, writes=[r_dn])
            P.op("act", lambda e: e.activation(dn[:], dn[:], AF.Ln), reads=[r_dn], writes=[r_dn])
            P.op("act", lambda e: e.activation(dn[:], dn[:], AF.Exp, scale=-1.0), reads=[r_dn], writes=[r_dn])
            P.op("dve", lambda e, bo_=bo_, qsl=qsl: e.tensor_tensor(of[:, :, qsl], bo_.rearrange("p (c q) -> p c q", c=8), dn[:],
                                                                    ALU.mult), reads=ro_ + [r_dn] + AR, writes=r_of)
        P.op("pool", lambda e: e.tensor_copy(kbuf[:, :, 0:128], kbuf[:, :, TW:TW + 128]), reads=[r_kbuf], writes=[r_kbuf])
        P.op("pool", lambda e: e.tensor_copy(vbuf[:, 0], vbuf[:, 4]), reads=[r_vbuf], writes=[r_vbuf])
        Wo = scr["awo"][0]
        for q2 in (range(2) if full else []):
            (wv,), wr = wload([Wo[:, :, q2 * 512:(q2 + 1) * 512]], "awo")
            for oc in range(4):
                m = q2 * 4 + oc
                bk, br = bank()
                for c in range(8):
                    mm(bk, wv[:, c, oc * 128:(oc + 1) * 128], of[:, c, :], c == 0, c == 7, [wr, r_of[c]], br)
                P.op("dve", lambda e, bk=bk, m=m: e.scalar_tensor_tensor(xT[:, m, :], bk, pfs("bo", m, m + 1), xT[:, m, :],
                                                                         ALU.add, ALU.add),
                     reads=br + [r_x[m], r_pf], writes=[r_x[m]])

    stages = [lambda ti, f: ffn(0, "ffn1"), mixer0, lambda ti, f: xattn(0), lambda ti, f: ffn(0, "ffn2"),
              lambda ti, f: ffn(1, "ffn1"), mixer1, lambda ti, f: xattn(1), lambda ti, f: ffn(1, "ffn2")]
    issue_conversions()
    issue_poolw()
    if split is None:
        mem_kv()
        for ti in range(NT):
            P.epoch = 1 + ti // EPT
            load_x(ti)
            for s_ in range(nstage):
                stages[s_](ti, ti == 0)
            store_x(ti, ti)
    else:
        npre, nmain = split
        assert NT == npre + 1 + nmain
        for ti in range(npre):
            P.epoch = 1 + ti // EPT
            load_x(ti)
            ffn(0, "ffn1")
            mixer0(ti, False, "lite_pool" if ti == npre - 1 else "lite")
        ti = npre
        P.epoch = 1 + ti // EPT
        mem_kv()
        load_x(ti)
        ffn(0, "ffn1")
        mixer0(ti, False)
        xattn(0)
        ffn(0, "ffn2")
        ffn(1, "ffn1")
        mixer1(ti, False, "kv")
        fl = cfs("flag")
        P.op("dve", lambda e: e.tensor_scalar(Hst[:], Hst[:], fl, None, ALU.mult), reads=[r_H, r_cf], writes=[r_H])
        P.op("act", lambda e: e.copy(Hb[:], Hst[:]), reads=[r_H], writes=[r_Hb])
        P.op("dve", lambda e: e.tensor_scalar(cpre[:].rearrange("p a b -> p (a b)"), cpre[:].rearrange("p a b -> p (a b)"), fl, None,
                                              ALU.mult), reads=[r_cpre, r_cf], writes=[r_cpre])
        P.op("dve", lambda e: e.tensor_scalar(phalo[:].rearrange("p a b -> p (a b)"), phalo[:].rearrange("p a b -> p (a b)"), fl,
                                              None, ALU.mult), reads=[r_phalo, r_cf], writes=[r_phalo])
        for k in range(nmain):
            ti = npre + 1 + k
            P.epoch = 1 + ti // EPT
            load_x(ti)
            for s_ in range(nstage):
                stages[s_](ti, k == 0)
            store_x(ti, k)
    P.emit()
    return nc


_CACHE = {}

WNAMES = ["ffn1_wi", "ffn1_wo", "ffn2_wi", "ffn2_wo", "xattn_wq", "xattn_wo"]


def make_in_map(inputs, xs, mem_b, first):
    m = {"x": np.ascontiguousarray(xs, dtype=np.float32), "mem": np.ascontiguousarray(mem_b, dtype=np.float32),
         "cst": make_consts(first), "prm": make_params(inputs)}
    for n in WNAMES:
        m[n] = np.ascontiguousarray(inputs[n], dtype=np.float32)
    m["ssd_in_proj"] = np.ascontiguousarray(inputs["ssd_in_proj"][0], dtype=np.float32)
    m["even_out_proj"] = np.ascontiguousarray(inputs["even_out_proj"][0], dtype=np.float32)
    m["attn_wqkv"] = np.ascontiguousarray(inputs["attn_wqkv"][0], dtype=np.float32)
    m["attn_wo"] = np.ascontiguousarray(inputs["attn_wo"][0], dtype=np.float32)
    m["mem_wkv"] = np.ascontiguousarray(inputs["mem_wkv"], dtype=np.float32)
    m["pool_w"] = np.ascontiguousarray(inputs["pool_w"][0], dtype=np.float32)
    return m


def kernel(**inputs):
    inputs = {k: np.asarray(v) for k, v in inputs.items()}
    x = inputs["x"]
    B, T, _ = x.shape
    HALF = T // 2
    NMAIN = HALF // TW
    NPRE = NMAIN - 1
    if "nc" not in _CACHE:
        _CACHE["nc"] = build(NPRE + 1 + NMAIN, split=(NPRE, NMAIN))
    nc = _CACHE["nc"]
    in_maps = []
    for c in range(8):
        b, half = c // 2, c % 2
        if half == 0:
            xs = np.concatenate([np.zeros((HALF, D), dtype=np.float32), x[b, :HALF]], axis=0)
        else:
            xs = x[b]
        in_maps.append(make_in_map(inputs, xs, inputs["mem"][b], half == 0))
    res = run_bass_kernel_spmd(nc, in_maps, core_ids=list(range(8)))
    out = np.empty((B, T, D), dtype=np.float32)
    for c in range(8):
        b, half = c // 2, c % 2
        out[b, half * HALF:(half + 1) * HALF] = res.results[c]["out"]
    return out
```

```python
import numpy as np
import concourse.bass as bass
import concourse.mybir as mybir
from concourse.bass_utils import run_bass_kernel_spmd

F32 = mybir.dt.float32
BF16 = mybir.dt.bfloat16
AF = mybir.ActivationFunctionType
ALU = mybir.AluOpType
AX = mybir.AxisListType

ENGS = ("pe", "act", "dve", "pool", "sp")
D = 1024
DFF = 2816
TW = 512
EPT = 2
EPS = 1e-6


class Res:
    __slots__ = ("name", "lw", "rd")

    def __init__(self, name):
        self.name = name
        self.lw = None
        self.rd = {}


class Op:
    __slots__ = ("eng", "fn", "deps", "dma", "sem", "cnt", "idx", "sig", "ord", "waits", "ep")


class Prog:
    def __init__(self, nc):
        self.nc = nc
        self.ops = {e: [] for e in ENGS}
        self.dsem = {}
        self.esem = {}
        self.final = []
        self.nres = 0
        self.auto = None
        self.epoch = 0

    def res(self, name=None):
        self.nres += 1
        return Res(name or f"r{self.nres}")

    def _deps(self, op, reads, writes):
        deps = []
        for r in reads:
            if r.lw is not None:
                deps.append(r.lw)
        for w in writes:
            if w.lw is not None:
                deps.append(w.lw)
            for o in w.rd.values():
                if isinstance(o, list):
                    deps.extend(o)
                else:
                    deps.append(o)
        op.deps = [d for d in deps if d is not op]
        for r in reads:
            if op.dma:
                r.rd.setdefault("dma", []).append(op)
            else:
                r.rd[op.eng] = op
        for w in writes:
            w.lw = op
            w.rd = {}

    def op(self, eng, fn, reads=(), writes=()):
        op = Op()
        op.eng = eng
        op.fn = fn
        op.dma = False
        op.sig = False
        op.sem = None
        op.ep = self.epoch
        op.idx = len(self.ops[eng])
        if self.auto is not None and self.auto not in writes:
            reads = list(reads) + [self.auto]
        self._deps(op, reads, writes)
        self.ops[eng].append(op)
        return op

    def dma(self, eng, out, in_, reads=(), writes=(), semkey=None, final=False, **kw):
        op = Op()
        op.eng = eng
        op.fn = lambda e: e.dma_start(out=out, in_=in_, **kw)
        op.dma = True
        op.sig = False
        op.ep = self.epoch
        op.idx = len(self.ops[eng])
        key = (semkey if semkey is not None else (writes[0] if writes else "misc"), self.epoch)
        if key not in self.dsem:
            self.dsem[key] = [self.nc.alloc_semaphore(f"d{len(self.dsem)}"), 0]
        ent = self.dsem[key]
        ent[1] += 16
        op.sem = ent[0]
        op.cnt = ent[1]
        self._deps(op, reads, writes)
        self.ops[eng].append(op)
        if final:
            self.final.append(op)
        return op

    def emit(self):
        nc = self.nc
        for e in ENGS:
            waited = {}
            for op in self.ops[e]:
                ws = []
                for d in op.deps:
                    if d.dma:
                        k = ("d", id(d.sem))
                        if waited.get(k, 0) < d.cnt:
                            waited[k] = d.cnt
                            ws.append(d)
                    else:
                        if d.eng == "pe" and e == "pe" and not op.dma:
                            continue
                        k = ("e", d.eng)
                        if waited.get(k, -1) < d.idx:
                            waited[k] = d.idx
                            d.sig = True
                            ws.append(d)
                op.waits = ws
        for e in ENGS:
            c = {}
            for op in self.ops[e]:
                if not op.dma and op.sig:
                    c[op.ep] = c.get(op.ep, 0) + 1
                    op.ord = c[op.ep]
                    if (e, op.ep) not in self.esem:
                        self.esem[(e, op.ep)] = nc.alloc_semaphore(f"e_{e}_{op.ep}")
        fin = list(self.final)
        ops = self.ops
        esem = self.esem

        def run(e, eng):
            for op in ops[e]:
                need = {}
                for d in op.waits:
                    if d.dma:
                        k = id(d.sem)
                        if k not in need or need[k][1] < d.cnt:
                            need[k] = (d.sem, d.cnt)
                    else:
                        k = (d.eng, d.ep)
                        if k not in need or need[k][1] < d.ord:
                            need[k] = (esem[k], d.ord)
                for sem, v in need.values():
                    eng.wait_ge(sem, v)
                ins = op.fn(eng)
                if op.dma:
                    ins.then_inc(op.sem, 16)
                elif op.sig:
                    ins.then_inc(esem[(e, op.ep)], 1)
            if e == "sp":
                seen = {}
                for d in fin:
                    seen[id(d.sem)] = (d.sem, max(d.cnt, seen.get(id(d.sem), (None, 0))[1]))
                for sem, v in seen.values():
                    eng.wait_ge(sem, v)

        with nc.Block() as block:
            @block.sync
            def _(eng):
                run("sp", eng)

            @block.tensor
            def _(eng):
                run("pe", eng)

            @block.scalar
            def _(eng):
                run("act", eng)

            @block.vector
            def _(eng):
                run("dve", eng)

            @block.gpsimd
            def _(eng):
                run("pool", eng)


CF = {}
_o = 0
for _n, _w in [("ident", 128), ("ones", 128), ("bones", 128), ("oh0", 128), ("oh1", 128), ("tri", 128), ("ustr", 128),
               ("negd", 256), ("negd1", 256), ("invc", 64), ("flag", 1)]:
    CF[_n] = (_o, _w)
    _o += _w
NCF = _o

PF = {}
_o = 0
for _n, _w in [("gffn1", 16), ("gmix", 16), ("gxat", 16), ("gffn2", 16), ("gmem", 8), ("gmemk", 2), ("gxq", 4),
               ("convw", 48), ("convb", 12), ("pscale", 8), ("bq", 8), ("bk", 2), ("bv", 256), ("gq", 1), ("gk", 1),
               ("bo", 8), ("sink", 8), ("dtb", 16), ("alog", 16), ("dexp", 1024), ("ssdn", 1024)]:
    PF[_n] = (_o, _w)
    _o += _w
NPF = _o

SLOPES = [2.0 ** (-8.0 * (i + 1) / 16) for i in range(16)]


def _qperm():
    idx = np.zeros((8, 128), dtype=np.int64)
    for a in range(2):
        for g in range(4):
            for half in range(2):
                head = (2 * a + half) * 4 + g
                idx[a * 4 + g, half * 64:(half + 1) * 64] = head * 64 + np.arange(64)
    return idx


def make_consts(first):
    c = np.zeros((128, NCF), dtype=np.float32)
    i = np.arange(128)
    c[:, CF["ident"][0]:CF["ident"][0] + 128] = np.eye(128)
    c[:, CF["ones"][0]:CF["ones"][0] + 128] = 1.0
    c[:, CF["bones"][0]:CF["bones"][0] + 128] = (i[:, None] // 64 == i[None, :] // 64)
    c[:, CF["tri"][0]:CF["tri"][0] + 128] = (i[:, None] <= i[None, :])
    c[:, CF["ustr"][0]:CF["ustr"][0] + 128] = (i[:, None] > i[None, :])
    BIG = -30000.0
    nd = np.zeros((128, 2, 128), dtype=np.float32)
    k = i[:, None]
    q = i[None, :]
    d0 = q + 128 - k
    nd[:, 0, :] = np.where(d0 < 128, -d0, BIG)
    d1 = q - k
    nd[:, 1, :] = np.where(d1 >= 0, -d1, BIG)
    c[:, CF["negd"][0]:CF["negd"][0] + 256] = nd.reshape(128, 256)
    nd1 = nd.copy()
    if first:
        nd1[:, 0, :] = BIG
    c[:, CF["negd1"][0]:CF["negd1"][0] + 256] = nd1.reshape(128, 256)
    inv = np.zeros((4, 16), dtype=np.float32)
    for kk, w in enumerate((2, 4, 8, 16)):
        t = np.arange(16)
        inv[kk] = 1.0 / (np.minimum(t + 1, w) if first else w)
    c[:, CF["invc"][0]:CF["invc"][0] + 64] = inv.reshape(1, 64)
    c[:, CF["flag"][0]] = 0.0 if first else 1.0
    c[:, CF["oh0"][0]:CF["oh0"][0] + 64] = 1.0
    c[:, CF["oh1"][0] + 64:CF["oh1"][0] + 128] = 1.0
    return c


def make_params(inp):
    p = np.zeros((128, NPF), dtype=np.float32)

    def put(name, arr):
        o, w = PF[name]
        arr = np.asarray(arr, dtype=np.float32)
        assert arr.shape == (128, w), (name, arr.shape)
        p[:, o:o + w] = arr

    def pp(v):
        v = np.asarray(v)
        return v.reshape(-1, 128).T

    put("gffn1", np.concatenate([pp(inp["ffn1_norm"][l]) for l in range(2)], axis=1))
    put("gmix", np.concatenate([pp(inp["mix_norm"][l]) for l in range(2)], axis=1))
    put("gxat", np.concatenate([pp(inp["xattn_norm"][l]) for l in range(2)], axis=1))
    put("gffn2", np.concatenate([pp(inp["ffn2_norm"][l]) for l in range(2)], axis=1))
    put("gmem", pp(inp["mem_norm"]))
    put("gmemk", pp(inp["mem_knorm"]))
    put("gxq", np.concatenate([pp(inp["xattn_qnorm"][l]) for l in range(2)], axis=1))
    cw = np.asarray(inp["ssd_conv_w"][0])
    put("convw", cw.T.reshape(12, 128, 4).transpose(1, 0, 2).reshape(128, 48))
    put("convb", pp(inp["ssd_conv_b"][0]))
    put("pscale", pp(inp["pool_scale"][0]))
    qi = _qperm()
    bqkv = np.asarray(inp["attn_bqkv"][0])
    put("bq", bqkv[qi].T)
    put("bk", pp(bqkv[1024:1280]))
    put("bv", np.tile(bqkv[1280:1536][None, :], (128, 1)))
    put("gq", np.tile(np.asarray(inp["attn_qnorm"][0]), 2)[:, None])
    put("gk", np.tile(np.asarray(inp["attn_knorm"][0]), 2)[:, None])
    put("bo", pp(inp["attn_bo"][0]))
    sk = np.asarray(inp["attn_sinks"][0])
    put("sink", sk[qi // 64].T)
    put("dtb", np.tile(np.asarray(inp["ssd_dt_bias"][0])[None, :], (128, 1)))
    put("alog", np.tile(np.asarray(inp["ssd_a_log"][0])[None, :], (128, 1)))
    put("dexp", np.tile(np.repeat(np.asarray(inp["ssd_d"][0]), 64)[None, :], (128, 1)))
    put("ssdn", np.tile(np.asarray(inp["ssd_norm"][0])[None, :], (128, 1)))
    return p


def build(NT, nstage=8, split=None):
    nc = bass.Bass("TRN2", target_bir_lowering=False)
    P = Prog(nc)
    NTOK = NT * TW

    def din(name, shape):
        return nc.dram_tensor(name, list(shape), F32, kind="ExternalInput").ap()

    x_d = din("x", [NTOK, D])
    mem_d = din("mem", [256, D])
    cst_d = din("cst", [128, NCF])
    prm_d = din("prm", [128, NPF])
    w_in = {
        "ffn1_wi": din("ffn1_wi", [2, D, 2 * DFF]), "ffn1_wo": din("ffn1_wo", [2, DFF, D]),
        "ffn2_wi": din("ffn2_wi", [2, D, 2 * DFF]), "ffn2_wo": din("ffn2_wo", [2, DFF, D]),
        "ssd_in_proj": din("ssd_in_proj", [D, 3600]), "even_out_proj": din("even_out_proj", [2048, D]),
        "attn_wqkv": din("attn_wqkv", [D, 1536]), "attn_wo": din("attn_wo", [D, D]),
        "xattn_wq": din("xattn_wq", [2, D, D]), "xattn_wo": din("xattn_wo", [2, D, D]),
        "mem_wkv": din("mem_wkv", [D, 2 * D]), "pool_w": din("pool_w", [4, 256, 256]),
    }
    NOUT = NTOK if split is None else split[1] * TW
    out_d = nc.dram_tensor("out", [NOUT, D], F32, kind="ExternalOutput").ap()

    def sb(name, shape, dt=F32):
        return nc.alloc_sbuf_tensor(name, list(shape), dt)

    scr = {}

    def mkscr(name, C, O):
        t = nc.dram_tensor("s_" + name, [128, C, O], BF16, kind="Internal").ap()
        scr[name] = (t, P.res("s_" + name))
        return t

    def mkscr2(name, J):
        t = nc.dram_tensor("s_" + name, [128, 8, J * 128], BF16, kind="Internal").ap()
        scr[name] = (t, P.res("s_" + name))
        return t

    def conv_w2(name, src, J):
        t, r = scr[name]
        for j in range(J):
            P.dma("pool", t[:, :, j * 128:(j + 1) * 128], src[j * 128:(j + 1) * 128, :].rearrange("p (m o) -> p m o", m=8),
                  writes=[r], semkey=r)

    def conv_w(name, src, C):
        t, r = scr[name]
        for c in range(C):
            P.dma("pool", t[:, c, :], src[c * 128:(c + 1) * 128, :], writes=[r], semkey=r)

    order = []
    for l in range(2):
        for f in ("ffn1", "ffn2"):
            mkscr(f"{f}_wi{l}", 8, 2 * DFF)
            mkscr2(f"{f}_wo{l}", 22)
    mkscr("wkv", 8, 2 * D)
    mkscr("inproj", 8, 3600)
    mkscr2("outproj", 16)
    mkscr("wqkv", 8, 1536)
    mkscr("awo", 8, D)
    for l in range(2):
        mkscr(f"xwq{l}", 8, D)
        mkscr(f"xwo{l}", 8, D)
    def issue_conversions():
        conv_w("ffn1_wi0", w_in["ffn1_wi"][0], 8)
        conv_w2("ffn1_wo0", w_in["ffn1_wo"][0], 22)
        conv_w("inproj", w_in["ssd_in_proj"], 8)
        conv_w("wkv", w_in["mem_wkv"], 8)
        conv_w2("outproj", w_in["even_out_proj"], 16)
        conv_w("xwq0", w_in["xattn_wq"][0], 8)
        conv_w("xwo0", w_in["xattn_wo"][0], 8)
        conv_w("ffn2_wi0", w_in["ffn2_wi"][0], 8)
        conv_w2("ffn2_wo0", w_in["ffn2_wo"][0], 22)
        conv_w("ffn1_wi1", w_in["ffn1_wi"][1], 8)
        conv_w2("ffn1_wo1", w_in["ffn1_wo"][1], 22)
        t_qkv, r_qkv = scr["wqkv"]
        for c in range(8):
            for a in range(2):
                for g in range(4):
                    srcq = w_in["attn_wqkv"][c * 128:(c + 1) * 128, a * 512:(a + 1) * 512].rearrange(
                        "p (hf g d) -> p hf g d", hf=2, g=4)[:, :, g, :]
                    dstq = t_qkv[:, c, (a * 4 + g) * 128:(a * 4 + g + 1) * 128].rearrange("p (hf d) -> p hf d", hf=2)
                    P.dma("pool", dstq, srcq, writes=[r_qkv], semkey=r_qkv)
            P.dma("pool", t_qkv[:, c, 1024:1536], w_in["attn_wqkv"][c * 128:(c + 1) * 128, 1024:1536], writes=[r_qkv], semkey=r_qkv)
        t_awo, r_awo = scr["awo"]
        for a in range(2):
            for g in range(4):
                for half in range(2):
                    head = (2 * a + half) * 4 + g
                    P.dma("pool", t_awo[half * 64:(half + 1) * 64, a * 4 + g, :],
                          w_in["attn_wo"][head * 64:(head + 1) * 64, :], writes=[r_awo], semkey=r_awo)
        conv_w("xwq1", w_in["xattn_wq"][1], 8)
        conv_w("xwo1", w_in["xattn_wo"][1], 8)
        conv_w("ffn2_wi1", w_in["ffn2_wi"][1], 8)
        conv_w2("ffn2_wo1", w_in["ffn2_wo"][1], 22)

    cf = sb("cf", [128, NCF])
    cbf = sb("cbf", [128, 640], BF16)
    pf = sb("pf", [128, NPF])
    r_cf, r_cbf, r_pf = P.res("cf"), P.res("cbf"), P.res("pf")
    P.dma("sp", cf[:], cst_d, writes=[r_cf])
    P.dma("sp", pf[:], prm_d, writes=[r_pf])
    P.op("dve", lambda e: e.tensor_copy(cbf[:], cf[:, 0:640]), reads=[r_cf], writes=[r_cbf])
    CONST = [r_cf, r_cbf, r_pf]

    def cfs(name, lo=0, hi=None):
        o, w = CF[name]
        return cf[:, o + lo:o + (w if hi is None else hi)]

    def cbs(name, lo=0, hi=None):
        o, w = CF[name]
        return cbf[:, o + lo:o + (w if hi is None else hi)]

    def pfs(name, lo=0, hi=None):
        o, w = PF[name]
        return pf[:, o + lo:o + (w if hi is None else hi)]

    poolw = sb("poolw", [128, 4, 2, 256], BF16)
    r_poolw = P.res("poolw")
    def issue_poolw():
        for k in range(4):
            P.dma("pool", poolw[:, k, :, :], w_in["pool_w"][k].rearrange("(ic p) o -> p ic o", p=128),
                  writes=[r_poolw], semkey=r_poolw)
    CONST.append(r_poolw)

    dv = sb("dv", [128, 64])
    r_dv = P.res("dv")
    P.op("act", lambda e: e.activation(dv[:, 0:16], pfs("alog"), AF.Exp), reads=[r_pf], writes=[r_dv])
    P.op("act", lambda e: e.activation(dv[:, 16:24], pfs("sink"), AF.Exp), reads=[r_pf], writes=[r_dv])
    P.op("dve", lambda e: e.tensor_scalar(dv[:, 0:16], dv[:, 0:16], -1.0, None, ALU.mult), reads=[r_dv], writes=[r_dv])
    P.op("dve", lambda e: e.tensor_scalar(dv[:, 24:28], pfs("gxq"), 1.0 / 16.0, None, ALU.mult), reads=[r_pf, r_dv],
         writes=[r_dv])
    P.op("dve", lambda e: e.tensor_scalar(dv[:, 28:29], pfs("gq"), 1.0 / 8.0, None, ALU.mult), reads=[r_pf, r_dv],
         writes=[r_dv])
    CONST.append(r_dv)

    ps = nc.alloc_psum_tensor("ps", [128, 4096], F32)
    pres = [P.res(f"bank{i}") for i in range(8)]
    bptr = [0]

    def bank(n=1):
        b = bptr[0]
        if b % n:
            b += n - b % n
        if b + n > 8:
            b = 0
        bptr[0] = (b + n) % 8
        return ps[:, b * 512:(b + n) * 512], pres[b:b + n]

    NSLOT = 3
    SLOTE = 5632
    slots = [sb(f"wslot{i}", [128, SLOTE], BF16) for i in range(NSLOT)]
    sres = [P.res(f"wslot{i}") for i in range(NSLOT)]
    sptr = [0]

    def wload(srcs, name):
        i = sptr[0]
        sptr[0] = (i + 1) % NSLOT
        views = []
        off = 0
        for s in srcs:
            a, b = s.shape[1], s.shape[2]
            v = slots[i][:, off:off + a * b].rearrange("p (a b) -> p a b", a=a)
            P.dma("sp", v, s, reads=[scr[name][1]], writes=[sres[i]])
            views.append(v)
            off += a * b
        assert off <= SLOTE
        return views, sres[i]

    xT = sb("xT", [128, 8, TW])
    r_x = [P.res(f"x{c}") for c in range(8)]
    h = sb("h", [128, 8, TW], BF16)
    r_h = [P.res(f"h{c}") for c in range(8)]
    sq = sb("sq", [128, 8, TW], BF16)
    r_sq = [P.res(f"sq{c}") for c in range(8)]
    rstd = sb("rstd", [128, TW])
    r_rstd = P.res("rstd")
    rstd2 = sb("rstd2", [128, TW])
    r_rstd2 = P.res("rstd2")
    rsp = [0]
    act = sb("act", [128, 22, TW], BF16)
    r_act = [P.res(f"act{j}") for j in range(22)]
    xio = act[:].rearrange("p j t -> p (j t)").bitcast(F32)[:, 0:4096].rearrange("p (j d) -> p j d", j=4)
    r_sgt = [P.res(f"sgt{i}") for i in range(2)]
    sgp = [0]

    def mm(out, lhsT, rhs, start, stop, reads, writes):
        P.op("pe", lambda e: e.matmul(out, lhsT, rhs, start=start, stop=stop), reads=reads, writes=writes)

    def norm_fm(src, src_res, C, N, groups, lhsT, dim, gain_fn, out_fn, out_res, extra_reads=()):
        for c in range(C):
            rr = [src_res[c]] if isinstance(src_res, list) else [src_res]
            if c % 2 == 1:
                P.op("dve", lambda e, c=c: e.tensor_tensor(sq[:, c, 0:N], src(c), src(c), ALU.mult), reads=rr, writes=[r_sq[c]])
            else:
                P.op("act", lambda e, c=c: e.activation(sq[:, c, 0:N], src(c), AF.Square), reads=rr, writes=[r_sq[c]])
        for g in groups:
            bk, br = bank()
            rsp[0] ^= 1
            rs_, r_rs = (rstd, r_rstd) if rsp[0] else (rstd2, r_rstd2)
            for i, c in enumerate(g):
                mm(bk[:, 0:N], lhsT, sq[:, c, 0:N], i == 0, i == len(g) - 1, [r_sq[c], r_cbf], br)
            P.op("act", lambda e, bk=bk, rs_=rs_: e.activation(rs_[:, 0:N], bk[:, 0:N], AF.Ln, bias=EPS, scale=1.0 / dim),
                 reads=br, writes=[r_rs])
            P.op("act", lambda e, rs_=rs_: e.activation(rs_[:, 0:N], rs_[:, 0:N], AF.Exp, scale=-0.5),
                 reads=[r_rs], writes=[r_rs])
            for c in g:
                P.op("dve", lambda e, c=c, rs_=rs_: e.scalar_tensor_tensor(out_fn(c), src(c), gain_fn(c), rs_[:, 0:N],
                                                                           ALU.mult, ALU.mult),
                     reads=[r_rs, r_pf, r_dv] + ([src_res[c]] if isinstance(src_res, list) else [src_res]) + list(extra_reads),
                     writes=[out_res[c]] if isinstance(out_res, list) else [out_res])

    def norm_x(gname, l):
        o = l * 8
        norm_fm(lambda c: xT[:, c, :], r_x, 8, TW, [list(range(8))], cbs("ones"), float(D),
                lambda c: pfs(gname, o + c, o + c + 1), lambda c: h[:, c, :], r_h)

    def ffn(l, f):
        phase_barrier()
        apos[0] = 0
        sgt = [carve(TW, F32) for _ in range(2)]
        norm_x("gffn1" if f == "ffn1" else "gffn2", l)
        wi_s = scr[f"{f}_wi{l}"][0]
        wo_s = scr[f"{f}_wo{l}"][0]
        for (j0, n) in [(0, 4), (4, 4), (8, 4), (12, 4), (16, 4), (20, 2)]:
            (gv,), gr = wload([wi_s[:, :, j0 * 128:(j0 + n) * 128]], f"{f}_wi{l}")
            (uv,), ur = wload([wi_s[:, :, DFF + j0 * 128:DFF + (j0 + n) * 128]], f"{f}_wi{l}")
            for jj in range(n):
                j = j0 + jj
                bg, rg = bank()
                bu, ru = bank()
                for c in range(8):
                    mm(bg, gv[:, c, jj * 128:(jj + 1) * 128], h[:, c, :], c == 0, c == 7, [gr, r_h[c]], rg)
                for c in range(8):
                    mm(bu, uv[:, c, jj * 128:(jj + 1) * 128], h[:, c, :], c == 0, c == 7, [ur, r_h[c]], ru)
                si = sgp[0]
                sgp[0] ^= 1
                P.op("act", lambda e, bg=bg, si=si: e.activation(sgt[si], bg, AF.Silu), reads=rg, writes=[r_sgt[si]])
                P.op("dve", lambda e, bu=bu, si=si, j=j: e.tensor_tensor(act[:, j, :], sgt[si], bu, ALU.mult),
                     reads=ru + [r_sgt[si]], writes=[r_act[j]])
        for m0 in range(0, 8, 2):
            (wv,), wr = wload([wo_s[:, m0:m0 + 2, :]], f"{f}_wo{l}")
            for mi in range(2):
                m = m0 + mi
                bo, ro = bank()
                for j in range(22):
                    mm(bo, wv[:, mi, j * 128:(j + 1) * 128], act[:, j, :], j == 0, j == 21, [wr, r_act[j]], ro)
                P.op("dve", lambda e, bo=bo, m=m: e.scalar_tensor_tensor(xT[:, m, :], bo, 0.5, xT[:, m, :], ALU.mult, ALU.add),
                     reads=ro + [r_x[m]], writes=[r_x[m]])

    def load_x(ti):
        P.dma("sp", xio, x_d[ti * TW:(ti + 1) * TW, :].rearrange("(j p) d -> p j d", p=128), reads=[], writes=r_act)
        for c in range(8):
            bk, br = bank()
            for j in range(4):
                P.op("pe", lambda e, bk=bk, c=c, j=j: e.transpose(bk[:, j * 128:(j + 1) * 128], xio[:, j, c * 128:(c + 1) * 128],
                                                                 cfs("ident")), reads=r_act + [r_cf], writes=br)
            eng = "act" if c % 2 else "dve"
            if eng == "act":
                P.op("act", lambda e, bk=bk, c=c: e.copy(xT[:, c, :], bk), reads=br, writes=[r_x[c]])
            else:
                P.op("dve", lambda e, bk=bk, c=c: e.tensor_copy(xT[:, c, :], bk), reads=br, writes=[r_x[c]])

    def store_x(ti, to):
        for j in range(4):
            for hf in range(2):
                bk, br = bank()
                for cc in range(4):
                    c = hf * 4 + cc
                    P.op("pe", lambda e, bk=bk, c=c, cc=cc, j=j: e.transpose(bk[:, cc * 128:(cc + 1) * 128],
                                                                            xT[:, c, j * 128:(j + 1) * 128], cfs("ident")),
                         reads=[r_x[c], r_cf], writes=br)
                if hf:
                    P.op("act", lambda e, bk=bk, j=j, hf=hf: e.copy(xio[:, j, hf * 512:(hf + 1) * 512], bk), reads=br,
                         writes=r_act)
                else:
                    P.op("dve", lambda e, bk=bk, j=j, hf=hf: e.tensor_copy(xio[:, j, hf * 512:(hf + 1) * 512], bk), reads=br,
                         writes=r_act)
        P.dma("sp", out_d[to * TW:(to + 1) * TW, :].rearrange("(j p) d -> p j d", p=128), xio, reads=r_act, writes=[],
              semkey="out", final=True)

    r_arena = P.res("arena")
    P.auto = r_arena

    def phase_barrier():
        P.op("dve", lambda e: e.memset(rstd[:, 0:1], 0.0), reads=[], writes=[r_arena, r_rstd])

    A0 = nc.alloc_sbuf_tensor("arena", [128, 36 * 1024], BF16)
    apos = [0]

    A1 = act[:].rearrange("p j t -> p (j t)")
    apos1 = [0]

    def carve(n_el, dt, reg=0):
        k = 2 if dt == F32 else 1
        pos, A, cap = (apos, A0, 36 * 1024) if reg == 0 else (apos1, A1, 22 * TW)
        o = pos[0]
        o = (o + 15) // 16 * 16
        pos[0] = o + n_el * k
        assert pos[0] <= cap, (reg, pos[0])
        v = A[:, o:o + n_el * k]
        return v.bitcast(F32) if dt == F32 else v

    memk = sb("memk", [128, 8, 256], BF16)
    memv = sb("memv", [128, 2, D], BF16)
    r_memk, r_memv = P.res("memk"), P.res("memv")

    def mem_kv():
        phase_barrier()
        apos[0] = 0
        memtm = xio[:, 0:2, :]
        P.dma("sp", memtm, mem_d.rearrange("(j p) d -> p j d", p=128), writes=r_act)
        mfm = carve(8 * 256, F32).rearrange("p (c m) -> p c m", c=8)
        for c in range(8):
            bk, br = bank()
            for j in range(2):
                P.op("pe", lambda e, bk=bk, c=c, j=j: e.transpose(bk[:, j * 128:(j + 1) * 128], xio[:, j, c * 128:(c + 1) * 128],
                                                                 cfs("ident")), reads=r_act + [r_cf], writes=br)
            P.op("dve", lambda e, bk=bk, c=c: e.tensor_copy(mfm[:, c, :], bk[:, 0:256]), reads=br + [r_arena], writes=[r_arena])
        hm = carve(8 * 256, BF16).rearrange("p (c m) -> p c m", c=8)
        norm_fm(lambda c: mfm[:, c, :], r_arena, 8, 256, [list(range(8))], cbs("ones"), float(D),
                lambda c: pfs("gmem", c, c + 1), lambda c: hm[:, c, :], r_arena)
        kf = carve(8 * 256, F32).rearrange("p (c m) -> p c m", c=8)
        wk = scr["wkv"][0]
        for q4 in range(4):
            (wv,), wr = wload([wk[:, :, q4 * 512:(q4 + 1) * 512]], "wkv")
            if q4 < 2:
                for oc in range(4):
                    ch = q4 * 4 + oc
                    bk, br = bank()
                    for c in range(8):
                        mm(bk[:, 0:256], wv[:, c, oc * 128:(oc + 1) * 128], hm[:, c, :], c == 0, c == 7, [wr, r_arena], br)
                    P.op("act", lambda e, bk=bk, ch=ch: e.copy(kf[:, ch, :], bk[:, 0:256]), reads=br + [r_arena], writes=[r_arena])
            else:
                for mc in range(2):
                    bk, br = bank()
                    for c in range(8):
                        mm(bk, hm[:, c, mc * 128:(mc + 1) * 128], wv[:, c, :], c == 0, c == 7, [wr, r_arena], br)
                    P.op("act", lambda e, bk=bk, mc=mc, q4=q4: e.copy(memv[:, mc, (q4 - 2) * 512:(q4 - 1) * 512], bk),
                         reads=br, writes=[r_memv])
        norm_fm(lambda c: kf[:, c, :], r_arena, 8, 256, [[2 * hh, 2 * hh + 1] for hh in range(4)], cbs("ones"), 256.0,
                lambda c: pfs("gmemk", c % 2, c % 2 + 1), lambda c: memk[:, c, :], r_memk)

    def xattn(l):
        phase_barrier()
        apos[0] = 0
        norm_x("gxat", l)
        qf = carve(8 * TW, F32).rearrange("p (c t) -> p c t", c=8)
        qn = carve(8 * TW, BF16).rearrange("p (c t) -> p c t", c=8)
        of = carve(8 * TW, BF16).rearrange("p (c t) -> p c t", c=8)
        eT = carve(2 * 2 * TW, BF16).rearrange("p (i m t) -> p i m t", i=2, m=2)
        rden = carve(2 * TW, F32).rearrange("p (i t) -> p i t", i=2)
        r_qf = [P.res() for _ in range(8)]
        r_qn = [P.res() for _ in range(8)]
        r_of = [P.res() for _ in range(8)]
        r_eT = [P.res() for _ in range(2)]
        r_rden = [P.res() for _ in range(2)]
        wq = scr[f"xwq{l}"][0]
        wo = scr[f"xwo{l}"][0]
        for q2 in range(2):
            (wv,), wr = wload([wq[:, :, q2 * 512:(q2 + 1) * 512]], f"xwq{l}")
            for oc in range(4):
                ch = q2 * 4 + oc
                bk, br = bank()
                for c in range(8):
                    mm(bk, wv[:, c, oc * 128:(oc + 1) * 128], h[:, c, :], c == 0, c == 7, [wr, r_h[c], r_arena], br)
                P.op("act", lambda e, bk=bk, ch=ch: e.copy(qf[:, ch, :], bk), reads=br + [r_arena], writes=[r_qf[ch]])
        norm_fm(lambda c: qf[:, c, :], r_qf, 8, TW, [[2 * hh, 2 * hh + 1] for hh in range(4)], cbs("ones"), 256.0,
                lambda c: dv[:, 24 + 2 * l + c % 2:25 + 2 * l + c % 2], lambda c: qn[:, c, :], r_qn)
        for hh in range(4):
            i = hh % 2
            for mc in range(2):
                bk, br = bank()
                for cc in range(2):
                    mm(bk, memk[:, 2 * hh + cc, mc * 128:(mc + 1) * 128], qn[:, 2 * hh + cc, :], cc == 0, cc == 1,
                       [r_memk, r_qn[2 * hh + cc]], br)
                P.op("act", lambda e, bk=bk, i=i, mc=mc: e.activation(eT[:, i, mc, :], bk, AF.Exp), reads=br + [r_arena],
                     writes=[r_eT[i]])
            bk, br = bank()
            for mc in range(2):
                mm(bk, cbs("ones"), eT[:, i, mc, :], mc == 0, mc == 1, [r_cbf, r_eT[i]], br)
            P.op("act", lambda e, bk=bk, i=i: e.activation(rden[:, i, :], bk, AF.Ln), reads=br + [r_arena], writes=[r_rden[i]])
            P.op("act", lambda e, i=i: e.activation(rden[:, i, :], rden[:, i, :], AF.Exp, scale=-1.0), reads=[r_rden[i]],
                 writes=[r_rden[i]])
            for dc in range(2):
                bk, br = bank()
                for mc in range(2):
                    mm(bk, memv[:, mc, hh * 256 + dc * 128:hh * 256 + (dc + 1) * 128], eT[:, i, mc, :], mc == 0, mc == 1,
                       [r_memv, r_eT[i]], br)
                P.op("dve", lambda e, bk=bk, i=i, c=2 * hh + dc: e.tensor_tensor(of[:, c, :], bk, rden[:, i, :], ALU.mult),
                     reads=br + [r_rden[i], r_arena], writes=[r_of[2 * hh + dc]])
        for q2 in range(2):
            (wv,), wr = wload([wo[:, :, q2 * 512:(q2 + 1) * 512]], f"xwo{l}")
            for oc in range(4):
                m = q2 * 4 + oc
                bk, br = bank()
                for c in range(8):
                    mm(bk, wv[:, c, oc * 128:(oc + 1) * 128], of[:, c, :], c == 0, c == 7, [wr, r_of[c]], br)
                P.op("dve", lambda e, bk=bk, m=m: e.tensor_tensor(xT[:, m, :], bk, xT[:, m, :], ALU.add),
                     reads=br + [r_x[m]], writes=[r_x[m]])

    Hst = sb("Hst", [128, D])
    Hb = sb("Hb", [128, D], BF16)
    r_H, r_Hb = P.res("H"), P.res("Hb")
    cpre = sb("cpre", [128, 12, 4])
    r_cpre = P.res("cpre")
    phalo = sb("phalo", [128, 8, 16])
    r_phalo = P.res("phalo")
    kbuf = sb("kbuf", [128, 2, 128 + TW], BF16)
    r_kbuf = P.res("kbuf")
    vbuf = sb("vbuf", [128, 5, 2, 2, 128], BF16)
    r_vbuf = P.res("vbuf")
    P.op("pool", lambda e: e.memset(Hst[:], 0.0), writes=[r_H])
    P.op("pool", lambda e: e.memset(Hb[:], 0.0), writes=[r_Hb])
    P.op("pool", lambda e: e.memset(cpre[:], 0.0), writes=[r_cpre])
    P.op("pool", lambda e: e.memset(phalo[:], 0.0), writes=[r_phalo])
    P.op("pool", lambda e: e.memset(kbuf[:], 0.0), writes=[r_kbuf])
    P.op("pool", lambda e: e.memset(vbuf[:], 0.0), writes=[r_vbuf])

    def mixer0(ti, first, mode="full"):
        full = mode == "full"
        phase_barrier()
        apos[0] = 0
        norm_x("gmix", 0)
        W = scr["inproj"][0]
        xbc = carve(12 * TW, BF16).rearrange("p (c t) -> p c t", c=12)
        pooled = carve(8 * TW, BF16).rearrange("p (c t) -> p c t", c=8)
        apos1[0] = 0
        mixin = carve(16 * TW, BF16, 1).rearrange("p (c t) -> p c t", c=16)
        szb = carve(4 * D, BF16).rearrange("p (j d) -> p j d", j=4)
        dtt = carve(4 * 16, F32).rearrange("p (j d) -> p j d", j=4)
        tcv = [carve(516, F32) for _ in range(2)]
        tacc = [carve(TW, F32) for _ in range(2)]
        tu = [carve(528, F32) for _ in range(3)]
        r_xbc = [P.res() for _ in range(12)]
        r_pooled = [P.res() for _ in range(8)]
        r_mixin = [P.res() for _ in range(16)]
        r_sz = [P.res() for _ in range(4)]
        r_dt = [P.res() for _ in range(4)]
        r_tcv = [P.res() for _ in range(2)]
        r_tacc = [P.res() for _ in range(2)]
        r_tu = [P.res() for _ in range(3)]
        AR = [r_arena]

        for g3 in range(3):
            (wv,), wr = wload([W[:, :, 1024 + g3 * 512:1024 + (g3 + 1) * 512]], "inproj")
            for oc in range(4):
                ch = g3 * 4 + oc
                bk, br = bank()
                for c in range(8):
                    mm(bk, wv[:, c, oc * 128:(oc + 1) * 128], h[:, c, :], c == 0, c == 7, [wr, r_h[c]] + AR, br)
                i = ch % 2
                P.op("act", lambda e, i=i, ch=ch: e.copy(tcv[i][:, 0:3], cpre[:, ch, 0:3]), reads=[r_cpre] + AR,
                     writes=[r_tcv[i]])
                P.op("act", lambda e, bk=bk, i=i: e.copy(tcv[i][:, 3:515], bk), reads=br + [r_tcv[i]], writes=[r_tcv[i]])
                P.op("dve", lambda e, i=i, ch=ch: e.tensor_copy(cpre[:, ch, 0:3], tcv[i][:, 512:515]), reads=[r_tcv[i]],
                     writes=[r_cpre])
                P.op("dve", lambda e, i=i, ch=ch: e.tensor_scalar(tacc[i][:], tcv[i][:, 0:512], pfs("convw", ch * 4, ch * 4 + 1),
                                                                  None, ALU.mult),
                     reads=[r_tcv[i], r_pf] + AR, writes=[r_tacc[i]])
                for tp in range(1, 4):
                    P.op("dve", lambda e, i=i, ch=ch, tp=tp: e.scalar_tensor_tensor(
                        tacc[i][:], tcv[i][:, tp:tp + 512], pfs("convw", ch * 4 + tp, ch * 4 + tp + 1), tacc[i][:],
                        ALU.mult, ALU.add), reads=[r_tcv[i], r_tacc[i], r_pf], writes=[r_tacc[i]])
                P.op("act", lambda e, i=i, ch=ch: e.activation(xbc[:, ch, :], tacc[i][:], AF.Silu,
                                                               bias=pfs("convb", ch, ch + 1)),
                     reads=[r_tacc[i], r_pf] + AR, writes=[r_xbc[ch]])

        if full:
            (wz0,), wzr0 = wload([W[:, :, 0:512]], "inproj")
            (wz1,), wzr1 = wload([W[:, :, 512:1024]], "inproj")
            for j in range(4):
                bk, br = bank(2)
                for hf, (wv, wr) in enumerate([(wz0, wzr0), (wz1, wzr1)]):
                    for c in range(8):
                        mm(bk[:, hf * 512:(hf + 1) * 512], h[:, c, j * 128:(j + 1) * 128], wv[:, c, :], c == 0, c == 7,
                           [wr, r_h[c]] + AR, [br[hf]])
                P.op("act", lambda e, bk=bk, j=j: e.activation(szb[:, j, :], bk, AF.Silu), reads=br + AR, writes=[r_sz[j]])
        (wdt,), wdtr = wload([W[:, :, 2560:2576]], "inproj")
        for j in range(4):
            bk, br = bank()
            for c in range(8):
                mm(bk[:, 0:16], h[:, c, j * 128:(j + 1) * 128], wdt[:, c, :], c == 0, c == 7, [wdtr, r_h[c]] + AR, br)
            P.op("dve", lambda e, bk=bk, j=j: e.tensor_tensor(dtt[:, j, :], bk[:, 0:16], pfs("dtb"), ALU.add),
                 reads=br + [r_pf] + AR, writes=[r_dt[j]])
            P.op("act", lambda e, j=j: e.activation(dtt[:, j, :], dtt[:, j, :], AF.Exp), reads=[r_dt[j]], writes=[r_dt[j]])
            P.op("act", lambda e, j=j: e.activation(dtt[:, j, :], dtt[:, j, :], AF.Ln, bias=1.0), reads=[r_dt[j]],
                 writes=[r_dt[j]])

        for g2 in (range(2) if mode != "lite" else []):
            (wv,), wr = wload([W[:, :, 2576 + g2 * 512:2576 + (g2 + 1) * 512]], "inproj")
            for oc in range(4):
                ch = g2 * 4 + oc
                k = ch // 2
                w = (2, 4, 8, 16)[k]
                bk, br = bank()
                for c in range(8):
                    mm(bk, wv[:, c, oc * 128:(oc + 1) * 128], h[:, c, :], c == 0, c == 7, [wr, r_h[c]] + AR, br)
                t0, t1, t2 = tu
                P.op("pool", lambda e, ch=ch: e.tensor_copy(tu[0][:, 0:16], phalo[:, ch, :]), reads=[r_phalo] + AR,
                     writes=[r_tu[0]])
                P.op("act", lambda e, bk=bk: e.copy(tu[0][:, 16:528], bk), reads=br + [r_tu[0]], writes=[r_tu[0]])
                P.op("pool", lambda e, ch=ch: e.tensor_copy(phalo[:, ch, :], tu[0][:, 512:528]), reads=[r_tu[0]],
                     writes=[r_phalo])
                if not full:
                    continue
                src, si = tu[0], 0
                sh = 1
                lo = 0
                while sh < w:
                    di = 1 if si != 1 else 2
                    dst = tu[di]
                    lo2 = lo + sh
                    P.op("pool", lambda e, dst=dst, src=src, lo2=lo2, sh=sh: e.tensor_tensor(
                        dst[:, lo2:528], src[:, lo2:528], src[:, lo2 - sh:528 - sh], ALU.add),
                        reads=[r_tu[si]] + AR, writes=[r_tu[di]])
                    src, si, lo = dst, di, lo2
                    sh *= 2
                P.op("dve", lambda e, src=src, ch=ch, w=w: e.scalar_tensor_tensor(
                    pooled[:, ch, :], src[:, 16:528], 1.0 / w, tu[0][:, 16:528], ALU.mult, ALU.subtract),
                    reads=[r_tu[si], r_tu[0]] + AR, writes=[r_pooled[ch]])
                if first:
                    ri = 3 - si
                    tmp = tu[ri]
                    P.op("pool", lambda e, src=src, tmp=tmp, k=k: e.tensor_tensor(
                        tmp[:, 0:16], src[:, 16:32], cfs("invc", k * 16, k * 16 + 16), ALU.mult),
                        reads=[r_tu[si], r_cf] + AR, writes=[r_tu[ri]])
                    P.op("pool", lambda e, tmp=tmp, ch=ch: e.tensor_tensor(
                        pooled[:, ch, 0:16], tmp[:, 0:16], tu[0][:, 16:32], ALU.subtract),
                        reads=[r_tu[ri], r_tu[0], r_pooled[ch]], writes=[r_pooled[ch]])
        for k in (range(4) if full else []):
            for oc in range(2):
                bk, br = bank()
                for ic in range(2):
                    mm(bk, poolw[:, k, ic, oc * 128:(oc + 1) * 128], pooled[:, 2 * k + ic, :], ic == 0, ic == 1,
                       [r_poolw, r_pooled[2 * k + ic]], br)
                ch = 2 * k + oc
                P.op("act", lambda e, bk=bk, ch=ch: e.activation(mixin[:, 8 + ch, :], bk, AF.Copy,
                                                                 scale=pfs("pscale", ch, ch + 1)),
                     reads=br + [r_pf] + AR, writes=[r_mixin[8 + ch]])

        xr = carve(D, BF16).rearrange("p (h d) -> p h d", h=16)
        xrd = carve(D, BF16).rearrange("p (h d) -> p h d", h=16)
        xtm = carve(D, F32)
        Btm = carve(256, BF16)
        sm = carve(128, F32)
        cbT = carve(256, F32).rearrange("p (g l) -> p g l", g=2)
        Rm = carve(8 * 128, F32).rearrange("p (h l) -> p h l", h=8)
        Em = carve(8 * 128, F32).rearrange("p (h l) -> p h l", h=8)
        MT = carve(16 * 128, BF16).rearrange("p (h l) -> p h l", h=16)
        t1 = carve(D, F32)
        t2 = carve(D, F32, 1)
        ytm = carve(D, BF16)
        ss = carve(8, F32)
        r_xr, r_xrd, r_xtm, r_Btm, r_sm, r_cbT, r_Rm, r_Em, r_MT, r_t1, r_t2, r_ytm, r_ss = [P.res() for _ in range(13)]
        a_neg = dv[:, 0:16]
        for j in range(4):
            tsl = slice(j * 128, (j + 1) * 128)
            bx, rx = bank(2)
            bxb = bx.bitcast(BF16)
            for c in range(8):
                P.op("pe", lambda e, c=c, tsl=tsl, bxb=bxb: e.transpose(bxb[:, c * 128:(c + 1) * 128], xbc[:, c, tsl], cbs("ident")),
                     reads=[r_xbc[c], r_cbf] + AR, writes=rx)
            for g in range(2):
                P.op("pe", lambda e, g=g, tsl=tsl, bxb=bxb: e.transpose(bxb[:, 1024 + g * 128:1024 + (g + 1) * 128],
                                                                        xbc[:, 8 + g, tsl], cbs("ident")),
                     reads=[r_xbc[8 + g], r_cbf] + AR, writes=rx)
            P.op("act", lambda e, bxb=bxb: e.copy(xtm[:], bxb[:, 0:1024]), reads=rx + AR, writes=[r_xtm])
            P.op("act", lambda e, bxb=bxb: e.copy(Btm[:], bxb[:, 1024:1280]), reads=rx + AR, writes=[r_Btm])
            P.op("dve", lambda e, j=j: e.tensor_tensor(sm[:, 80:96], dtt[:, j, :], a_neg, ALU.mult),
                 reads=[r_dt[j], r_dv] + AR, writes=[r_sm])
            bs, rs = bank()
            mm(bs[:, 0:16], cfs("tri"), sm[:, 80:96], True, True, [r_cf, r_sm], rs)
            mm(bs[:, 16:32], cfs("ones"), sm[:, 80:96], True, True, [r_cf, r_sm], rs)
            P.op("dve", lambda e, bs=bs: e.tensor_copy(sm[:, 0:32], bs[:, 0:32]), reads=rs + [r_sm], writes=[r_sm])
            P.op("dve", lambda e: e.tensor_tensor(sm[:, 32:48], sm[:, 16:32], sm[:, 0:16], ALU.subtract), reads=[r_sm],
                 writes=[r_sm])
            P.op("act", lambda e: e.activation(sm[:, 32:48], sm[:, 32:48], AF.Exp), reads=[r_sm], writes=[r_sm])
            P.op("act", lambda e: e.activation(sm[:, 48:64], sm[:, 0:16], AF.Exp), reads=[r_sm], writes=[r_sm])
            P.op("act", lambda e: e.activation(sm[:, 64:80], sm[:, 16:32], AF.Exp), reads=[r_sm], writes=[r_sm])
            xtm3 = xtm.rearrange("p (h d) -> p h d", h=16)
            P.op("dve", lambda e, j=j, xtm3=xtm3: e.tensor_tensor(xr[:], xtm3, dtt[:, j, :].unsqueeze(2).broadcast_to([128, 16, 64]),
                                                                  ALU.mult), reads=[r_xtm, r_dt[j]] + AR, writes=[r_xr])
            P.op("dve", lambda e: e.tensor_tensor(xrd[:], xr[:], sm[:, 32:48].unsqueeze(2).broadcast_to([128, 16, 64]), ALU.mult),
                 reads=[r_xr, r_sm], writes=[r_xrd])
            def state_update():
                bS, rS = bank(2)
                for g in range(2):
                    mm(bS[:, g * 512:(g + 1) * 512], Btm[:, g * 128:(g + 1) * 128], xrd[:, g * 8:(g + 1) * 8, :], True, True,
                       [r_Btm, r_xrd], [rS[g]])
                P.op("dve", lambda e: e.tensor_tensor(Hst[:].rearrange("p (h d) -> p h d", h=16),
                                                      Hst[:].rearrange("p (h d) -> p h d", h=16),
                                                      sm[:, 64:80].unsqueeze(2).broadcast_to([128, 16, 64]), ALU.mult),
                     reads=[r_H, r_sm], writes=[r_H])
                P.op("dve", lambda e, bS=bS: e.tensor_tensor(Hst[:], Hst[:], bS, ALU.add), reads=rS + [r_H], writes=[r_H])
                P.op("act", lambda e: e.copy(Hb[:], Hst[:]), reads=[r_H], writes=[r_Hb])

            if full:
                bc, rc = bank()
                for g in range(2):
                    mm(bc[:, g * 128:(g + 1) * 128], xbc[:, 8 + g, tsl], xbc[:, 10 + g, tsl], True, True,
                       [r_xbc[8 + g], r_xbc[10 + g]] + AR, rc)
                P.op("dve", lambda e, bc=bc: e.tensor_tensor(cbT[:], bc[:, 0:256].rearrange("p (g l) -> p g l", g=2),
                                                             cfs("tri").unsqueeze(1).broadcast_to([128, 2, 128]), ALU.mult),
                     reads=rc + [r_cf] + AR, writes=[r_cbT])
                bo_, ro_ = bank(2)
                for g in range(2):
                    mm(bo_[:, g * 512:(g + 1) * 512], xbc[:, 10 + g, tsl], Hb[:, g * 512:(g + 1) * 512], True, True,
                       [r_xbc[10 + g], r_Hb] + AR, [ro_[g]])
                P.op("dve", lambda e, bo_=bo_: e.tensor_tensor(t1.rearrange("p (h d) -> p h d", h=16),
                                                               bo_.rearrange("p (h d) -> p h d", h=16),
                                                               sm[:, 48:64].unsqueeze(2).broadcast_to([128, 16, 64]), ALU.mult),
                     reads=ro_ + [r_sm] + AR, writes=[r_t1])
                state_update()
                for g in range(2):
                    P.op("dve", lambda e, g=g: e.tensor_tensor(Rm[:], cfs("tri").unsqueeze(1).broadcast_to([128, 8, 128]),
                                                               sm[:, 80 + g * 8:88 + g * 8].unsqueeze(2).broadcast_to([128, 8, 128]),
                                                               ALU.mult), reads=[r_cf, r_sm] + AR, writes=[r_Rm])
                    bd, rd = bank(2)
                    for hf in range(2):
                        mm(bd[:, hf * 512:(hf + 1) * 512], cfs("ustr"), Rm[:, hf * 4:(hf + 1) * 4, :], True, True, [r_cf, r_Rm], [rd[hf]])
                    P.op("act", lambda e, bd=bd: e.activation(Em.rearrange("p h l -> p (h l)"), bd, AF.Exp), reads=rd + AR,
                         writes=[r_Em])
                    P.op("dve", lambda e, g=g: e.tensor_tensor(MT[:, g * 8:(g + 1) * 8, :], Em[:],
                                                               cbT[:, g, :].unsqueeze(1).broadcast_to([128, 8, 128]), ALU.mult),
                         reads=[r_Em, r_cbT], writes=[r_MT])
                by, ry = bank(2)
                for hh in range(16):
                    mm(by[:, hh * 64:(hh + 1) * 64], MT[:, hh, :], xr[:, hh, :], True, True, [r_MT, r_xr], [ry[hh // 8]])
                P.op("dve", lambda e, by=by: e.tensor_tensor(t1[:], t1[:], by, ALU.add), reads=ry + [r_t1], writes=[r_t1])
                P.op("pool", lambda e: e.tensor_tensor(t2[:], xtm[:], pfs("dexp"), ALU.mult), reads=[r_xtm, r_pf] + AR, writes=[r_t2])
                P.op("dve", lambda e: e.tensor_tensor(t1[:], t1[:], t2[:], ALU.add), reads=[r_t1, r_t2], writes=[r_t1])
                P.op("dve", lambda e, j=j: e.tensor_tensor(t1[:], t1[:], szb[:, j, :], ALU.mult), reads=[r_t1, r_sz[j]], writes=[r_t1])
                P.op("dve", lambda e: e.memset(ss[:, 0:2], 0.0), reads=[r_ss], writes=[r_ss])
                for g in range(2):
                    P.op("act", lambda e, g=g: e.activation(t2[:, g * 512:(g + 1) * 512], t1[:, g * 512:(g + 1) * 512], AF.Square,
                                                            accum_out=ss[:, g:g + 1]), reads=[r_t1, r_t2], writes=[r_t2, r_ss])
                P.op("act", lambda e: e.activation(ss[:, 2:4], ss[:, 0:2], AF.Ln, bias=EPS, scale=1.0 / 512.0), reads=[r_ss], writes=[r_ss])
                P.op("act", lambda e: e.activation(ss[:, 2:4], ss[:, 2:4], AF.Exp, scale=-0.5), reads=[r_ss], writes=[r_ss])
                for g in range(2):
                    P.op("dve", lambda e, g=g: e.scalar_tensor_tensor(ytm[:, g * 512:(g + 1) * 512], t1[:, g * 512:(g + 1) * 512],
                                                                      ss[:, 2 + g:3 + g], pfs("ssdn", g * 512, (g + 1) * 512),
                                                                      ALU.mult, ALU.mult),
                         reads=[r_t1, r_ss, r_pf], writes=[r_ytm])
                bt, rt = bank()
                btb = bt.bitcast(BF16)
                for c in range(8):
                    P.op("pe", lambda e, c=c, btb=btb: e.transpose(btb[:, c * 128:(c + 1) * 128], ytm[:, c * 128:(c + 1) * 128],
                                                                   cbs("ident")), reads=[r_ytm, r_cbf], writes=rt)
                P.op("act", lambda e, btb=btb, tsl=tsl: e.copy(mixin[:, 0:8, tsl], btb[:, 0:1024].rearrange("p (c t) -> p c t", c=8)),
                     reads=rt + AR, writes=[r_mixin[c] for c in range(8)])
            if not full:
                state_update()

        if full:
            Wo = scr["outproj"][0]
            for m0 in range(0, 8, 2):
                (wv,), wr = wload([Wo[:, m0:m0 + 2, :]], "outproj")
                for mi in range(2):
                    m = m0 + mi
                    bk, br = bank()
                    for c in range(16):
                        mm(bk, wv[:, mi, c * 128:(c + 1) * 128], mixin[:, c, :], c == 0, c == 15, [wr, r_mixin[c]], br)
                    P.op("dve", lambda e, bk=bk, m=m: e.tensor_tensor(xT[:, m, :], bk, xT[:, m, :], ALU.add),
                         reads=br + [r_x[m]], writes=[r_x[m]])

    QPERM = _qperm()

    def mixer1(ti, first, mode="full"):
        full = mode == "full"
        phase_barrier()
        apos[0] = 0
        norm_x("gmix", 1)
        AR = [r_arena]
        W = scr["wqkv"][0]
        qf = carve(8 * TW, F32).rearrange("p (c t) -> p c t", c=8)
        qn = carve(8 * TW, BF16).rearrange("p (c t) -> p c t", c=8)
        kf = carve(2 * TW, F32).rearrange("p (c t) -> p c t", c=2)
        of = carve(8 * TW, BF16).rearrange("p (c t) -> p c t", c=8)
        st = [carve(512, F32).rearrange("p (i k q) -> p i k q", i=2, k=2) for _ in range(2)]
        eT = [carve(512, BF16).rearrange("p (i k q) -> p i k q", i=2, k=2) for _ in range(2)]
        dn = carve(1024, F32).rearrange("p (c q) -> p c q", c=8)
        r_qf = [P.res() for _ in range(8)]
        r_qn = [P.res() for _ in range(8)]
        r_kf = [P.res() for _ in range(2)]
        r_of = [P.res() for _ in range(8)]
        r_st = [P.res() for _ in range(2)]
        r_eT = [P.res() for _ in range(2)]
        r_dn = P.res()
        r_dnh = [P.res() for _ in range(2)]
        if full:
            (w0,), wr0 = wload([W[:, :, 0:512]], "wqkv")
            (w1,), wr1 = wload([W[:, :, 512:1024]], "wqkv")
        (w2,), wr2 = wload([W[:, :, 1024:1536]], "wqkv")
        for a in (range(2) if full else []):
            for g in range(4):
                ch = a * 4 + g
                bk, br = bank()
                wv, wr = (w0, wr0) if a == 0 else (w1, wr1)
                for c in range(8):
                    mm(bk, wv[:, c, g * 128:(g + 1) * 128], h[:, c, :], c == 0, c == 7, [wr, r_h[c]], br)
                P.op("act", lambda e, bk=bk, ch=ch: e.activation(qf[:, ch, :], bk, AF.Identity, bias=pfs("bq", ch, ch + 1)),
                     reads=br + [r_pf] + AR, writes=[r_qf[ch]])
        for kc in range(2):
            bk, br = bank()
            for c in range(8):
                mm(bk, w2[:, c, kc * 128:(kc + 1) * 128], h[:, c, :], c == 0, c == 7, [wr2, r_h[c]] + AR, br)
            P.op("act", lambda e, bk=bk, kc=kc: e.activation(kf[:, kc, :], bk, AF.Identity, bias=pfs("bk", kc, kc + 1)),
                 reads=br + [r_pf] + AR, writes=[r_kf[kc]])
        for j in range(4):
            bk, br = bank()
            for c in range(8):
                mm(bk[:, 0:256], h[:, c, j * 128:(j + 1) * 128], w2[:, c, 256:512], c == 0, c == 7, [wr2, r_h[c]] + AR, br)
            for a in range(2):
                for half in range(2):
                    kv = 2 * a + half
                    P.op("dve", lambda e, bk=bk, j=j, a=a, half=half, kv=kv: e.tensor_tensor(
                        vbuf[:, 1 + j, a, half, half * 64:(half + 1) * 64], bk[:, kv * 64:(kv + 1) * 64],
                        pfs("bv", kv * 64, (kv + 1) * 64), ALU.add), reads=br + [r_pf, r_vbuf], writes=[r_vbuf])
        if full:
            norm_fm(lambda c: qf[:, c, :], r_qf, 8, TW, [[c] for c in range(8)], cbs("bones"), 64.0,
                    lambda c: dv[:, 28:29], lambda c: qn[:, c, :], r_qn)
        norm_fm(lambda c: kf[:, c, :], r_kf, 2, TW, [[c] for c in range(2)], cbs("bones"), 64.0,
                lambda c: pfs("gk"), lambda c: kbuf[:, c, 128:128 + TW], r_kbuf, extra_reads=[r_kbuf])
        for j in (range(4) if full else []):
            qsl = slice(j * 128, (j + 1) * 128)
            nd = cfs("negd1") if (first and j == 0) else cfs("negd")
            bo_, ro_ = ps[:, 0:1024], pres[0:2]
            bd_, rd_ = ps[:, 1024:2048], pres[2:4]
            for a in range(2):
                for g in range(4):
                    ch = a * 4 + g
                    i = ch % 2
                    bsh = [ps[:, (4 + 2 * i + hf) * 512:(5 + 2 * i + hf) * 512] for hf in range(2)]
                    rsh = [[pres[4 + 2 * i + hf]] for hf in range(2)]
                    for half in range(2):
                        for kb in range(2):
                            P.op("pe", lambda e, half=half, kb=kb, a=a, ch=ch, j=j, qsl=qsl, bsx=bsh[half]: e.matmul(
                                bsx[:, kb * 128:(kb + 1) * 128],
                                kbuf[half * 64:(half + 1) * 64, a, (j + kb) * 128:(j + kb + 1) * 128],
                                qn[half * 64:(half + 1) * 64, ch, qsl], start=True, stop=True),
                                reads=[r_kbuf, r_qn[ch]] + AR, writes=rsh[half])
                    for half in range(2):
                        head = (2 * a + half) * 4 + g
                        P.op("dve", lambda e, i=i, half=half, head=head, nd=nd, bsx=bsh[half]: e.scalar_tensor_tensor(
                            st[i][:, half, :, :], nd.rearrange("p (k q) -> p k q", k=2), float(SLOPES[head]),
                            bsx[:, 0:256].rearrange("p (k q) -> p k q", k=2), ALU.mult, ALU.add),
                            reads=rsh[half] + [r_cf] + AR, writes=[r_st[i]])
                    P.op("act", lambda e, i=i: e.activation(eT[i][:], st[i][:], AF.Exp), reads=[r_st[i]] + AR, writes=[r_eT[i]])
                    n = 0
                    for half in range(2):
                        for kb in range(2):
                            mm(bo_[:, ch * 128:(ch + 1) * 128], vbuf[:, j + kb, a, half, :], eT[i][:, half, kb, :], n == 0, n == 3,
                               [r_vbuf, r_eT[i]], [ro_[ch // 4]])
                            n += 1
                    n = 0
                    for half in range(2):
                        for kb in range(2):
                            mm(bd_[:, ch * 128:(ch + 1) * 128], cbs("oh1") if half else cbs("oh0"), eT[i][:, half, kb, :],
                               n == 0, n == 3, [r_cbf, r_eT[i]], [rd_[ch // 4]])
                            n += 1
                    if g == 3:
                        hb = a
                        P.op("dve", lambda e, bd_=bd_, hb=hb: e.tensor_tensor(
                            dn[:, hb * 4:(hb + 1) * 4, :], bd_[:, hb * 512:(hb + 1) * 512].rearrange("p (c q) -> p c q", c=4),
                            dv[:, 16 + hb * 4:20 + hb * 4].unsqueeze(2).broadcast_to([128, 4, 128]), ALU.add),
                            reads=[rd_[hb], r_dv] + AR, writes=[r_dnh[hb]])
                        P.op("act", lambda e, hb=hb: e.activation(dn[:, hb * 4:(hb + 1) * 4, :], dn[:, hb * 4:(hb + 1) * 4, :], AF.Ln),
                             reads=[r_dnh[hb]], writes=[r_dnh[hb]])
                        P.op("act", lambda e, hb=hb: e.activation(dn[:, hb * 4:(hb + 1) * 4, :], dn[:, hb * 4:(hb + 1) * 4, :], AF.Exp,
                                                                  scale=-1.0), reads=[r_dnh[hb]], writes=[r_dnh[hb]])
                        P.op("dve", lambda e, bo_=bo_, qsl=qsl, hb=hb: e.tensor_tensor(
                            of[:, hb * 4:(hb + 1) * 4, qsl], bo_[:, hb * 512:(hb + 1) * 512].rearrange("p (c q) -> p c q", c=4),
                            dn[:, hb * 4:(hb + 1) * 4, :], ALU.mult),
                            reads=[ro_[hb], r_dnh[hb]] + AR, writes=r_of[hb * 4:(hb + 1) * 4])
        P.op("pool", lambda e: e.tensor_copy(kbuf[:, :, 0:128], kbuf[:, :, TW:TW + 128]), reads=[r_kbuf], writes=[r_kbuf])
        P.op("pool", lambda e: e.tensor_copy(vbuf[:, 0], vbuf[:, 4]), reads=[r_vbuf], writes=[r_vbuf])
        Wo = scr["awo"][0]
        for q2 in (range(2) if full else []):
            (wv,), wr = wload([Wo[:, :, q2 * 512:(q2 + 1) * 512]], "awo")
            for oc in range(4):
                m = q2 * 4 + oc
                bk, br = bank()
                for c in range(8):
                    mm(bk, wv[:, c, oc * 128:(oc + 1) * 128], of[:, c, :], c == 0, c == 7, [wr, r_of[c]], br)
                P.op("dve", lambda e, bk=bk, m=m: e.scalar_tensor_tensor(xT[:, m, :], bk, pfs("bo", m, m + 1), xT[:, m, :],
                                                                         ALU.add, ALU.add),
                     reads=br + [r_x[m], r_pf], writes=[r_x[m]])

    stages = [lambda ti, f: ffn(0, "ffn1"), mixer0, lambda ti, f: xattn(0), lambda ti, f: ffn(0, "ffn2"),
              lambda ti, f: ffn(1, "ffn1"), mixer1, lambda ti, f: xattn(1), lambda ti, f: ffn(1, "ffn2")]
    issue_conversions()
    issue_poolw()
    if split is None:
        mem_kv()
        for ti in range(NT):
            P.epoch = 1 + ti // EPT
            load_x(ti)
            for s_ in range(nstage):
                stages[s_](ti, ti == 0)
            store_x(ti, ti)
    else:
        npre, nmain = split
        assert NT == npre + 1 + nmain
        for ti in range(npre):
            P.epoch = 1 + ti // EPT
            load_x(ti)
            ffn(0, "ffn1")
            mixer0(ti, False, "lite_pool" if ti == npre - 1 else "lite")
        ti = npre
        P.epoch = 1 + ti // EPT
        mem_kv()
        load_x(ti)
        ffn(0, "ffn1")
        mixer0(ti, False)
        xattn(0)
        ffn(0, "ffn2")
        ffn(1, "ffn1")
        mixer1(ti, False, "kv")
        fl = cfs("flag")
        P.op("dve", lambda e: e.tensor_scalar(Hst[:], Hst[:], fl, None, ALU.mult), reads=[r_H, r_cf], writes=[r_H])
        P.op("act", lambda e: e.copy(Hb[:], Hst[:]), reads=[r_H], writes=[r_Hb])
        P.op("dve", lambda e: e.tensor_scalar(cpre[:].rearrange("p a b -> p (a b)"), cpre[:].rearrange("p a b -> p (a b)"), fl, None,
                                              ALU.mult), reads=[r_cpre, r_cf], writes=[r_cpre])
        P.op("dve", lambda e: e.tensor_scalar(phalo[:].rearrange("p a b -> p (a b)"), phalo[:].rearrange("p a b -> p (a b)"), fl,
                                              None, ALU.mult), reads=[r_phalo, r_cf], writes=[r_phalo])
        for k in range(nmain):
            ti = npre + 1 + k
            P.epoch = 1 + ti // EPT
            load_x(ti)
            for s_ in range(nstage):
                stages[s_](ti, k == 0)
            store_x(ti, k)
    P.emit()
    return nc


_CACHE = {}

WNAMES = ["ffn1_wi", "ffn1_wo", "ffn2_wi", "ffn2_wo", "xattn_wq", "xattn_wo"]


def make_in_map(inputs, xs, mem_b, first):
    m = {"x": np.ascontiguousarray(xs, dtype=np.float32), "mem": np.ascontiguousarray(mem_b, dtype=np.float32),
         "cst": make_consts(first), "prm": make_params(inputs)}
    for n in WNAMES:
        m[n] = np.ascontiguousarray(inputs[n], dtype=np.float32)
    m["ssd_in_proj"] = np.ascontiguousarray(inputs["ssd_in_proj"][0], dtype=np.float32)
    m["even_out_proj"] = np.ascontiguousarray(inputs["even_out_proj"][0], dtype=np.float32)
    m["attn_wqkv"] = np.ascontiguousarray(inputs["attn_wqkv"][0], dtype=np.float32)
    m["attn_wo"] = np.ascontiguousarray(inputs["attn_wo"][0], dtype=np.float32)
    m["mem_wkv"] = np.ascontiguousarray(inputs["mem_wkv"], dtype=np.float32)
    m["pool_w"] = np.ascontiguousarray(inputs["pool_w"][0], dtype=np.float32)
    return m


def kernel(**inputs):
    inputs = {k: np.asarray(v) for k, v in inputs.items()}
    x = inputs["x"]
    B, T, _ = x.shape
    HALF = T // 2
    NMAIN = HALF // TW
    NPRE = NMAIN - 1
    if "nc" not in _CACHE:
        _CACHE["nc"] = build(NPRE + 1 + NMAIN, split=(NPRE, NMAIN))
    nc = _CACHE["nc"]
    in_maps = []
    for c in range(8):
        b, half = c // 2, c % 2
        if half == 0:
            xs = np.concatenate([np.zeros((HALF, D), dtype=np.float32), x[b, :HALF]], axis=0)
        else:
            xs = x[b]
        in_maps.append(make_in_map(inputs, xs, inputs["mem"][b], half == 0))
    res = run_bass_kernel_spmd(nc, in_maps, core_ids=list(range(8)))
    out = np.empty((B, T, D), dtype=np.float32)
    for c in range(8):
        b, half = c // 2, c % 2
        out[b, half * HALF:(half + 1) * HALF] = res.results[c]["out"]
    return out
```
